# Optimizing a Trainium2 kernel written in Bass

```python
import math
import jax, jax.numpy as jnp
from jax import lax
import numpy as np


D_MODEL = 1024
BATCH = 16
SEQ = 2048
DEPTH = 2

N_MIXERS = 4
HEADS = 4
HEAD_DIM = D_MODEL // (N_MIXERS * HEADS)
GROUP_WIDTH = HEADS * HEAD_DIM
MIX_WIDTH = N_MIXERS * GROUP_WIDTH
D_FF = 4 * D_MODEL
LN_EPS = 1e-5
NEG = -1e30

M_QK_DIM = HEAD_DIM // 2
M_CHUNK = 64
M_CONV = 4

DIL_PATTERNS = ((128, 1), (512, 4), (2048, 16))

IDX_HEADS = 4
IDX_DIM = 64
DSA_TOPK = 256
DSA_QBLOCK = 128

NSA_CMP_LEN = 32
NSA_CMP_STRIDE = 16
NSA_SEL_LEN = 64
NSA_TOPN = 16
NSA_WINDOW = 512
NSA_CMP_HIDDEN = 256
NSA_QBLOCK = 64
NSA_FORCE = 1e9

NUM_BUCKETS = 32
MAX_DISTANCE = 128
N_BIAS_HEADS = 3 * HEADS

ALPHA = (2 * DEPTH) ** 0.25
BETA = (8 * DEPTH) ** -0.25

IN_SPLITS = (
    ('a_q', HEADS * M_QK_DIM), ('a_k', HEADS * M_QK_DIM), ('a_v', GROUP_WIDTH),
    ('a_i', HEADS), ('a_f', HEADS), ('a_o', GROUP_WIDTH),
    ('b_q', GROUP_WIDTH), ('b_k', GROUP_WIDTH), ('b_v', GROUP_WIDTH),
    ('c_q', GROUP_WIDTH), ('c_k', HEAD_DIM), ('c_v', HEAD_DIM),
    ('c_iq', IDX_HEADS * IDX_DIM), ('c_ik', IDX_DIM), ('c_iw', IDX_HEADS),
    ('d_q', GROUP_WIDTH), ('d_kc', HEAD_DIM), ('d_vc', HEAD_DIM),
    ('d_ks', HEAD_DIM), ('d_vs', HEAD_DIM), ('d_kw', HEAD_DIM), ('d_vw', HEAD_DIM),
    ('d_g', 3 * HEADS),
)
D_IN = sum(w for _, w in IN_SPLITS)

kernel_name = 'hybrid_parallel_mixers_deepnorm'


def _col_range(name):
    off = 0
    for n, w in IN_SPLITS:
        if n == name:
            return off, off + w
        off += w
    raise KeyError(name)


def split_cols(z):
    out = {}
    off = 0
    for n, w in IN_SPLITS:
        out[n] = z[..., off:off + w]
        off += w
    return out


def layer_norm(x, g, b):
    xf = x.astype(jnp.float32)
    mu = jnp.mean(xf, axis=-1, keepdims=True)
    var = jnp.mean(jnp.square(xf - mu), axis=-1, keepdims=True)
    return ((xf - mu) * lax.rsqrt(var + LN_EPS) * g + b).astype(x.dtype)


def t5_bucket(dist):
    n = jnp.maximum(dist, 0)
    max_exact = NUM_BUCKETS // 2
    nf = jnp.maximum(n, max_exact).astype(jnp.float32)
    large = max_exact + (jnp.log(nf / max_exact) / math.log(MAX_DISTANCE / max_exact)
                         * (NUM_BUCKETS - max_exact)).astype(jnp.int32)
    large = jnp.minimum(large, NUM_BUCKETS - 1)
    return jnp.where(n < max_exact, n, large)


def masked_softmax(logits, mask):
    logits = jnp.where(mask, logits.astype(jnp.float32), NEG)
    m = jnp.max(logits, axis=-1, keepdims=True)
    p = jnp.where(mask, jnp.exp(logits - m), 0.0)
    den = jnp.maximum(jnp.sum(p, axis=-1, keepdims=True), 1e-30)
    return p / den, (m + jnp.log(den))[..., 0]


def gather_rows(t, idx):
    return jax.vmap(lambda tt, ii: tt[ii])(t, idx)


def causal_conv(x, w):
    c = x.shape[-1]
    return lax.conv_general_dilated(
        x, w[:, None, :].astype(x.dtype), window_strides=(1,),
        padding=((w.shape[0] - 1, 0),), dimension_numbers=('NWC', 'WIO', 'NWC'),
        feature_group_count=c)


def mlstm_mixer(q, k, v, i_pre, f_pre, o_pre, norm_g):
    B, S, H, DK = q.shape
    DV = v.shape[-1]
    nc = S // M_CHUNK
    f32 = jnp.float32

    def chunks(t):
        t = t.astype(f32).reshape((B, nc, M_CHUNK) + t.shape[2:])
        return jnp.transpose(t, (1, 0, 3, 2) + tuple(range(4, t.ndim)))

    qc = chunks(q)
    kc = chunks(k * (DK ** -0.5))
    vc = chunks(v)
    ic = chunks(i_pre)
    fc = chunks(jax.nn.log_sigmoid(f_pre.astype(f32)))
    tri = jnp.tril(jnp.ones((M_CHUNK, M_CHUNK), dtype=bool))

    def step(carry, xs):
        C, n, m = carry
        qb, kb, vb, ib, fb = xs
        b = jnp.cumsum(fb, axis=-1)
        D = jnp.where(tri, b[..., :, None] - b[..., None, :] + ib[..., None, :], NEG)
        inter = b + m[..., None]
        m_t = jnp.maximum(inter, jnp.max(D, axis=-1))
        sc = jnp.einsum('bhtd,bhsd->bhts', qb, kb) * jnp.exp(D - m_t[..., None])
        wi = jnp.exp(inter - m_t)
        num = jnp.einsum('bhts,bhsv->bhtv', sc, vb) + wi[..., None] * jnp.einsum('bhtd,bhdv->bhtv', qb, C)
        den = jnp.sum(sc, axis=-1) + wi * jnp.einsum('bhtd,bhd->bht', qb, n)
        h = num / jnp.maximum(jnp.abs(den), jnp.exp(-m_t))[..., None]
        bL = b[..., -1]
        g = bL[..., None] - b + ib
        m_new = jnp.maximum(bL + m, jnp.max(g, axis=-1))
        ws = jnp.exp(g - m_new[..., None])
        wc = jnp.exp(bL + m - m_new)
        C = wc[..., None, None] * C + jnp.einsum('bhs,bhsd,bhsv->bhdv', ws, kb, vb)
        n = wc[..., None] * n + jnp.einsum('bhs,bhsd->bhd', ws, kb)
        return (C, n, m_new), h

    init = (jnp.zeros((B, H, DK, DV), f32), jnp.zeros((B, H, DK), f32), jnp.zeros((B, H), f32))
    _, hc = lax.scan(step, init, (qc, kc, vc, ic, fc))
    h = jnp.transpose(hc, (1, 0, 3, 2, 4)).reshape(B, S, H, DV)
    h = jax.nn.sigmoid(o_pre.astype(f32)).reshape(B, S, H, DV) * h
    mu = jnp.mean(h, axis=-1, keepdims=True)
    var = jnp.mean(jnp.square(h - mu), axis=-1, keepdims=True)
    h = (h - mu) * lax.rsqrt(var + LN_EPS)
    return (h.reshape(B, S, H * DV) * norm_g).astype(v.dtype)


def dilated_branch(q, k, v, bias_tab, window, dil):
    B, S, H, hd = q.shape
    W = window // dil
    L = S // dil
    nb = -(-L // W)
    Lp = nb * W

    def residues(t):
        t = jnp.transpose(t.reshape(B, L, dil, H, hd), (0, 2, 1, 3, 4))
        t = jnp.pad(t, ((0, 0), (0, 0), (0, Lp - L), (0, 0), (0, 0)))
        return t.reshape(B, dil, nb, W, H, hd)

    qb, kb, vb = residues(q), residues(k), residues(v)
    shift = lambda t: jnp.pad(t, ((0, 0), (0, 0), (1, 0), (0, 0), (0, 0), (0, 0)))[:, :, :-1]
    kk = jnp.concatenate([shift(kb), kb], axis=3)
    vv = jnp.concatenate([shift(vb), vb], axis=3)
    qi = jnp.arange(W)[:, None]
    ki = jnp.arange(2 * W)[None, :]
    j = W + qi - ki
    band = (j >= 0) & (j <= W)
    first_ok = (jnp.arange(nb)[:, None, None] > 0) | (ki[None] >= W)
    mask = (band[None] & first_ok)[None, None, :, None]
    bias = jnp.transpose(bias_tab[t5_bucket(j * dil)], (2, 0, 1))
    logits = jnp.einsum('brnqhd,brnkhd->brnhqk', qb, kk).astype(jnp.float32) * hd ** -0.5 + bias
    p, lse = masked_softmax(logits, mask)
    o = jnp.einsum('brnhqk,brnkhd->brnqhd', p, vv.astype(jnp.float32))
    o = jnp.transpose(o.reshape(B, dil, Lp, H, hd)[:, :, :L], (0, 2, 1, 3, 4)).reshape(B, S, H, hd)
    lse = jnp.transpose(lse, (0, 1, 2, 4, 3)).reshape(B, dil, Lp, H)[:, :, :L]
    lse = jnp.transpose(lse, (0, 2, 1, 3)).reshape(B, S, H)
    return o, lse


def dilated_mixer(q, k, v, bias_tab):
    B, S, H, hd = q.shape
    outs, lses = [], []
    for window, dil in DIL_PATTERNS:
        o, lse = dilated_branch(q, k, v, bias_tab, window, dil)
        outs.append(o)
        lses.append(lse)
    wts = jax.nn.softmax(jnp.stack(lses, axis=0), axis=0)
    o = jnp.sum(wts[..., None] * jnp.stack(outs, axis=0), axis=0)
    return o.reshape(B, S, H * hd).astype(q.dtype)


def dsa_mixer(q, k, v, iq, ik, iw, bias_tab):
    B, S, H, hd = q.shape
    topk = min(DSA_TOPK, S // 4)
    kpos = jnp.arange(S)
    iw = iw.astype(jnp.float32) * (IDX_HEADS * IDX_DIM) ** -0.5

    def block(i):
        s0 = i * DSA_QBLOCK
        tq = s0 + jnp.arange(DSA_QBLOCK)
        qb = lax.dynamic_slice_in_dim(q, s0, DSA_QBLOCK, axis=1)
        iqb = lax.dynamic_slice_in_dim(iq, s0, DSA_QBLOCK, axis=1)
        iwb = lax.dynamic_slice_in_dim(iw, s0, DSA_QBLOCK, axis=1)
        rel = jax.nn.relu(jnp.einsum('bqhd,bsd->bqhs', iqb, ik).astype(jnp.float32))
        score = jnp.einsum('bqhs,bqh->bqs', rel, iwb)
        score = jnp.where((kpos[None, :] <= tq[:, None])[None], score, NEG)
        _, idx = lax.top_k(score, topk)
        kg = gather_rows(k, idx)
        vg = gather_rows(v, idx)
        dist = tq[None, :, None] - idx
        bias = jnp.transpose(bias_tab[t5_bucket(dist)], (0, 3, 1, 2))
        logits = jnp.einsum('bqhd,bqkd->bhqk', qb, kg).astype(jnp.float32) * hd ** -0.5 + bias
        p, _ = masked_softmax(logits, (dist >= 0)[:, None])
        return jnp.einsum('bhqk,bqkd->bqhd', p, vg.astype(jnp.float32))

    out = lax.map(block, jnp.arange(S // DSA_QBLOCK))
    return jnp.transpose(out, (1, 0, 2, 3, 4)).reshape(B, S, H * hd).astype(q.dtype)


def nsa_compress(t, pos, w1, w2):
    B, S, hd = t.shape
    n_cmp = (S - NSA_CMP_LEN) // NSA_CMP_STRIDE + 1
    idx = jnp.arange(n_cmp)[:, None] * NSA_CMP_STRIDE + jnp.arange(NSA_CMP_LEN)[None, :]
    blocks = (t[:, idx] + pos).reshape(B, n_cmp, NSA_CMP_LEN * hd)
    return jax.nn.silu(blocks @ w1) @ w2


def nsa_mixer(q, kc, vc, ks, vs, kw, vw, gates, cmp_pos, cmp_w1, cmp_w2, bias_tab):
    B, S, H, hd = q.shape
    f32 = jnp.float32
    scale = hd ** -0.5
    kcmp = nsa_compress(kc, cmp_pos[0], cmp_w1[0], cmp_w2[0])
    vcmp = nsa_compress(vc, cmp_pos[1], cmp_w1[1], cmp_w2[1]).astype(f32)
    n_cmp = kcmp.shape[1]
    cmp_start = jnp.arange(n_cmp) * NSA_CMP_STRIDE
    cmp_end = cmp_start + NSA_CMP_LEN - 1
    n_sel = S // NSA_SEL_LEN
    topn = min(NSA_TOPN, n_sel)
    sel_start = jnp.arange(n_sel) * NSA_SEL_LEN
    overlap = jnp.clip(jnp.minimum(cmp_start[:, None] + NSA_CMP_LEN, sel_start[None] + NSA_SEL_LEN)
                       - jnp.maximum(cmp_start[:, None], sel_start[None]), 0).astype(f32) / NSA_CMP_LEN
    kw_pad = jnp.pad(kw, ((0, 0), (NSA_WINDOW, 0), (0, 0)))
    vw_pad = jnp.pad(vw, ((0, 0), (NSA_WINDOW, 0), (0, 0)))
    g = jax.nn.sigmoid(gates.astype(f32)).reshape(B, S, H, 3)
    jj = jnp.arange(n_sel)

    def block(i):
        s0 = i * NSA_QBLOCK
        tq = s0 + jnp.arange(NSA_QBLOCK)
        qb = lax.dynamic_slice_in_dim(q, s0, NSA_QBLOCK, axis=1)
        dist_c = tq[:, None] - cmp_end[None]
        bias_c = jnp.transpose(bias_tab[t5_bucket(dist_c)], (2, 0, 1))[None]
        lc = jnp.einsum('bqhd,bcd->bhqc', qb, kcmp).astype(f32) * scale + bias_c
        pc, _ = masked_softmax(lc, (dist_c >= 0)[None, None])
        o_c = jnp.einsum('bhqc,bcd->bqhd', pc, vcmp)
        imp = jnp.einsum('bhqc,cj->bqj', pc, overlap)
        cur = tq // NSA_SEL_LEN
        forced = (jj[None] == 0) | (jj[None] == cur[:, None]) | (jj[None] == cur[:, None] - 1)
        admissible = sel_start[None] <= tq[:, None]
        imp = jnp.where(forced[None], NSA_FORCE, imp)
        imp = jnp.where(admissible[None], imp, NEG)
        _, sel = lax.top_k(imp, topn)
        tok = (sel[..., None] * NSA_SEL_LEN + jnp.arange(NSA_SEL_LEN)).reshape(B, NSA_QBLOCK, topn * NSA_SEL_LEN)
        ksg = gather_rows(ks, tok)
        vsg = gather_rows(vs, tok)
        dist_s = tq[None, :, None] - tok
        bias_s = jnp.transpose(bias_tab[t5_bucket(dist_s)], (0, 3, 1, 2))
        ls = jnp.einsum('bqhd,bqkd->bhqk', qb, ksg).astype(f32) * scale + bias_s
        ps, _ = masked_softmax(ls, (dist_s >= 0)[:, None])
        o_s = jnp.einsum('bhqk,bqkd->bqhd', ps, vsg.astype(f32))
        kwb = lax.dynamic_slice_in_dim(kw_pad, s0, NSA_WINDOW + NSA_QBLOCK, axis=1)
        vwb = lax.dynamic_slice_in_dim(vw_pad, s0, NSA_WINDOW + NSA_QBLOCK, axis=1)
        kpos = s0 - NSA_WINDOW + jnp.arange(NSA_WINDOW + NSA_QBLOCK)
        dist_w = tq[:, None] - kpos[None]
        valid_w = (dist_w >= 0) & (dist_w < NSA_WINDOW) & (kpos[None] >= 0)
        bias_w = jnp.transpose(bias_tab[t5_bucket(dist_w)], (2, 0, 1))[None]
        lw = jnp.einsum('bqhd,bkd->bhqk', qb, kwb).astype(f32) * scale + bias_w
        pw, _ = masked_softmax(lw, valid_w[None, None])
        o_w = jnp.einsum('bhqk,bkd->bqhd', pw, vwb.astype(f32))
        gb = lax.dynamic_slice_in_dim(g, s0, NSA_QBLOCK, axis=1)
        return gb[..., 0:1] * o_c + gb[..., 1:2] * o_s + gb[..., 2:3] * o_w

    out = lax.map(block, jnp.arange(S // NSA_QBLOCK))
    return jnp.transpose(out, (1, 0, 2, 3, 4)).reshape(B, S, H * hd).astype(q.dtype)


def hybrid_layer(x, w_in, b_in, a_conv, a_norm, d_cmp_pos, d_cmp_w1, d_cmp_w2, w_out, b_out,
                 ln1_g, ln1_b, w_ff1, b_ff1, w_ff2, b_ff2, ln2_g, ln2_b, rel_bias):
    B, S, _ = x.shape
    H, hd = HEADS, HEAD_DIM
    z = split_cols(jnp.einsum('bsd,de->bse', x, w_in) + b_in)
    qk = jax.nn.silu(causal_conv(jnp.concatenate([z['a_q'], z['a_k']], axis=-1), a_conv))
    a_q, a_k = jnp.split(qk, 2, axis=-1)
    out_a = mlstm_mixer(a_q.reshape(B, S, H, M_QK_DIM), a_k.reshape(B, S, H, M_QK_DIM),
                        z['a_v'].reshape(B, S, H, hd), z['a_i'], z['a_f'], z['a_o'], a_norm)
    out_b = dilated_mixer(z['b_q'].reshape(B, S, H, hd), z['b_k'].reshape(B, S, H, hd),
                          z['b_v'].reshape(B, S, H, hd), rel_bias[:, 0:H])
    out_c = dsa_mixer(z['c_q'].reshape(B, S, H, hd), z['c_k'], z['c_v'],
                      z['c_iq'].reshape(B, S, IDX_HEADS, IDX_DIM), z['c_ik'], z['c_iw'],
                      rel_bias[:, H:2 * H])
    out_d = nsa_mixer(z['d_q'].reshape(B, S, H, hd), z['d_kc'], z['d_vc'], z['d_ks'], z['d_vs'],
                      z['d_kw'], z['d_vw'], z['d_g'], d_cmp_pos, d_cmp_w1, d_cmp_w2,
                      rel_bias[:, 2 * H:3 * H])
    mixed = jnp.concatenate([out_a, out_b, out_c, out_d], axis=-1).astype(x.dtype)
    x = layer_norm(ALPHA * x + (mixed @ w_out + b_out), ln1_g, ln1_b)
    ff = jnp.square(jax.nn.relu(x @ w_ff1 + b_ff1)) @ w_ff2 + b_ff2
    return layer_norm(ALPHA * x + ff, ln2_g, ln2_b)


def setup_inputs(seed: int = 0) -> dict:
    key = jax.random.key(seed)
    ks = jax.random.split(key, 20)
    f32 = jnp.float32

    def nrm(k, shape, scale):
        return scale * jax.random.normal(k, shape, f32)

    x = nrm(ks[0], (BATCH, SEQ, D_MODEL), 1.0)
    w_in = nrm(ks[1], (DEPTH, D_MODEL, D_IN), D_MODEL ** -0.5)
    f0, f1 = _col_range('a_f')
    b_in = nrm(ks[2], (DEPTH, D_IN), 0.02)
    b_in = b_in.at[:, f0:f1].set(3.0 + 3.0 * jax.random.uniform(ks[3], (DEPTH, f1 - f0), f32))
    a_conv = nrm(ks[4], (DEPTH, M_CONV, 2 * HEADS * M_QK_DIM), M_CONV ** -0.5)
    a_norm = 1.0 + nrm(ks[5], (DEPTH, GROUP_WIDTH), 0.02)
    d_cmp_pos = nrm(ks[6], (DEPTH, 2, NSA_CMP_LEN, HEAD_DIM), 0.02)
    d_cmp_w1 = nrm(ks[7], (DEPTH, 2, NSA_CMP_LEN * HEAD_DIM, NSA_CMP_HIDDEN), (NSA_CMP_LEN * HEAD_DIM) ** -0.5)
    d_cmp_w2 = nrm(ks[8], (DEPTH, 2, NSA_CMP_HIDDEN, HEAD_DIM), NSA_CMP_HIDDEN ** -0.5)
    w_out = nrm(ks[9], (DEPTH, MIX_WIDTH, D_MODEL), BETA * MIX_WIDTH ** -0.5)
    b_out = nrm(ks[10], (DEPTH, D_MODEL), 0.02)
    ln1_g = 1.0 + nrm(ks[11], (DEPTH, D_MODEL), 0.02)
    ln1_b = nrm(ks[12], (DEPTH, D_MODEL), 0.02)
    w_ff1 = nrm(ks[13], (DEPTH, D_MODEL, D_FF), D_MODEL ** -0.5)
    b_ff1 = nrm(ks[14], (DEPTH, D_FF), 0.02)
    w_ff2 = nrm(ks[15], (DEPTH, D_FF, D_MODEL), BETA * D_FF ** -0.5)
    b_ff2 = nrm(ks[16], (DEPTH, D_MODEL), 0.02)
    ln2_g = 1.0 + nrm(ks[17], (DEPTH, D_MODEL), 0.02)
    ln2_b = nrm(ks[18], (DEPTH, D_MODEL), 0.02)
    rel_bias = nrm(ks[19], (NUM_BUCKETS, N_BIAS_HEADS), 0.2)
    return {'x': x, 'w_in': w_in, 'b_in': b_in, 'a_conv': a_conv, 'a_norm': a_norm,
            'd_cmp_pos': d_cmp_pos, 'd_cmp_w1': d_cmp_w1, 'd_cmp_w2': d_cmp_w2,
            'w_out': w_out, 'b_out': b_out, 'ln1_g': ln1_g, 'ln1_b': ln1_b,
            'w_ff1': w_ff1, 'b_ff1': b_ff1, 'w_ff2': w_ff2, 'b_ff2': b_ff2,
            'ln2_g': ln2_g, 'ln2_b': ln2_b, 'rel_bias': rel_bias}


def reference(x, w_in, b_in, a_conv, a_norm, d_cmp_pos, d_cmp_w1, d_cmp_w2, w_out, b_out,
              ln1_g, ln1_b, w_ff1, b_ff1, w_ff2, b_ff2, ln2_g, ln2_b, rel_bias):
    h = x
    for l in range(DEPTH):
        h = hybrid_layer(h, w_in[l], b_in[l], a_conv[l], a_norm[l], d_cmp_pos[l], d_cmp_w1[l],
                         d_cmp_w2[l], w_out[l], b_out[l], ln1_g[l], ln1_b[l], w_ff1[l], b_ff1[l],
                         w_ff2[l], b_ff2[l], ln2_g[l], ln2_b[l], rel_bias)
    return h
```

```python
import math
import numpy as np
from contextlib import ExitStack
import concourse.bass as bass
import concourse.mybir as mybir
from concourse.bass_utils import run_bass_kernel_spmd

F32 = mybir.dt.float32
BF16 = mybir.dt.bfloat16
AF = mybir.ActivationFunctionType
ALU = mybir.AluOpType
AX = mybir.AxisListType

S = 2048
D = 1024
NT = 16
DFF = 4096
ALPHA = (2 * 2) ** 0.25
LN_EPS = 1e-5
IN_SPLITS = (('a_q', 128), ('a_k', 128), ('a_v', 256), ('a_i', 4), ('a_f', 4), ('a_o', 256),
             ('b_q', 256), ('b_k', 256), ('b_v', 256),
             ('c_q', 256), ('c_k', 64), ('c_v', 64), ('c_iq', 256), ('c_ik', 64), ('c_iw', 4),
             ('d_q', 256), ('d_kc', 64), ('d_vc', 64), ('d_ks', 64), ('d_vs', 64), ('d_kw', 64),
             ('d_vw', 64), ('d_g', 12))
COL = {}
_o = 0
for _n, _w in IN_SPLITS:
    COL[_n] = _o
    _o += _w
D_IN = _o


class _Op:
    __slots__ = ("eng", "dma", "fn", "deps", "signals", "sem", "val", "nd", "ring", "bg")

    def __init__(self, eng, dma, fn):
        self.eng = eng
        self.dma = dma
        self.fn = fn
        self.deps = []
        self.signals = dma
        self.sem = None
        self.val = 0
        self.nd = 0
        self.ring = ""
        self.bg = False


class Prog:
    ENGS = ("pe", "act", "dve", "pool", "sp")
    KD = 8

    def __init__(self, nc, es):
        self.nc = nc
        self.pending = {e: [] for e in self.ENGS}
        self.lastw = {}
        self.readers = {}
        self.dmas = {}
        self.lastc = {e: None for e in self.ENGS}
        self.cnt = {e: 0 for e in self.ENGS}
        self.seen = {e: {} for e in self.ENGS}
        self.csem = {}
        self.dsem = {}
        self.nops = 0
        for e in self.ENGS:
            self.csem[e] = es.enter_context(nc.semaphore("c_" + e))
            for i in range(self.KD):
                self.dsem[(e, "", i)] = es.enter_context(nc.semaphore("d_%s%d" % (e, i)))
        for i in range(self.KD):
            self.dsem[("pool", "bg", i)] = es.enter_context(nc.semaphore("dbg_%d" % i))

    def op(self, eng, fn, r=(), w=(), dma=False, ring="", after=(), bg=False):
        o = _Op(eng, dma, fn)
        o.ring = ring
        o.bg = bg
        self.nops += 1
        deps = {}

        def add(d, raw):
            if d is None:
                return
            if (not d.dma) and (not dma) and d.eng == eng:
                if eng == "pe" or not raw:
                    return
            deps[id(d)] = d

        for k in r:
            add(self.lastw.get(k), True)
        for k in w:
            add(self.lastw.get(k), False)
            for d in self.readers.get(k, ()):
                add(d, False)
        for d in after:
            deps[id(d)] = d
        if dma:
            lst = self.dmas.setdefault((eng, ring), [])
            o.nd = len(lst)
            if o.nd >= self.KD:
                d = lst[o.nd - self.KD]
                deps[id(d)] = d
            lst.append(o)
        else:
            self.lastc[eng] = o
        for d in deps.values():
            d.signals = True
        o.deps = list(deps.values())
        for k in r:
            lst = self.readers.setdefault(k, [])
            lst.append(o)
            if len(lst) > 16:
                keep = {}
                for x in lst:
                    keep[(x.eng, x.dma, x.ring, x.nd % self.KD if x.dma else 0)] = x
                self.readers[k] = list(keep.values())
        for k in w:
            self.lastw[k] = o
            self.readers[k] = []
        self.pending[eng].append(o)
        return o

    def barrier(self, all_rings=False):
        deps = []
        for e in self.ENGS:
            if self.lastc[e] is not None:
                deps.append(self.lastc[e])
        for (e, ring), lst in self.dmas.items():
            if ring == "" or all_rings:
                deps.extend(lst[-self.KD:])
        for d in deps:
            d.signals = True
        for e in self.ENGS:
            o = _Op(e, False, lambda x: x.nop())
            o.deps = list(deps)
            self.pending[e].append(o)
            self.lastc[e] = o
        self.lastw = {}
        self.readers = {}

    def flush(self):
        nc = self.nc
        for e in self.ENGS:
            for o in self.pending[e]:
                if o.dma:
                    o.sem = self.dsem[(e, o.ring, o.nd % self.KD)]
                    o.val = 16 * (o.nd // self.KD + 1)
                elif o.signals:
                    self.cnt[e] += 1
                    o.sem = self.csem[e]
                    o.val = self.cnt[e]
        pend = self.pending
        self.pending = {e: [] for e in self.ENGS}
        with nc.Block() as block:
            def run(e, engobj):
                seen = self.seen[e]
                for o in pend[e]:
                    for d in o.deps:
                        key = id(d.sem)
                        if seen.get(key, 0) >= d.val:
                            continue
                        engobj.wait_ge(d.sem, d.val)
                        seen[key] = d.val
                    ins = o.fn(engobj)
                    if o.signals:
                        ins.then_inc(o.sem, 16 if o.dma else 1)

            @block.tensor
            def _(x):
                run("pe", x)

            @block.scalar
            def _(x):
                run("act", x)

            @block.vector
            def _(x):
                run("dve", x)

            @block.gpsimd
            def _(x):
                run("pool", x)

            @block.sync
            def _(x):
                run("sp", x)

    def dma(self, out, in_, r=(), w=(), eng="sp", ring="", after=(), **kw):
        return self.op(eng, lambda e: e.dma_start(out=out, in_=in_, **kw), r, w, dma=True, ring=ring, after=after)

    def matmul(self, out, lhsT, rhs, start=True, stop=True, r=(), w=(), sgc=False):
        if sgc:
            return self.op("pe", lambda e: e.matmul(out, lhsT, rhs, start=start, stop=stop, skip_group_check=True), r, w)
        return self.op("pe", lambda e: e.matmul(out, lhsT, rhs, start=start, stop=stop), r, w)

    def transpose(self, out, in_, ident, r=(), w=()):
        return self.op("pe", lambda e: e.transpose(out, in_, ident), r, w)

    def activation(self, out, in_, func, bias=None, scale=None, r=(), w=(), accum_out=None):
        kw = {}
        if bias is not None:
            kw["bias"] = bias
        if scale is not None:
            kw["scale"] = scale
        if accum_out is not None:
            kw["accum_out"] = accum_out
        return self.op("act", lambda e: e.activation(out, in_, func, **kw), r, w)

    def tt(self, eng, out, in0, in1, op, r=(), w=(), after=()):
        return self.op(eng, lambda e: e.tensor_tensor(out, in0, in1, op), r, w, after=after)

    def ts(self, eng, out, in0, s1, s2, op0, op1=None, r=(), w=(), accum_out=None):
        kw = {}
        if accum_out is not None:
            kw["accum_out"] = accum_out
        if op1 is None:
            return self.op(eng, lambda e: e.tensor_scalar(out, in0, s1, s2, op0, **kw), r, w)
        return self.op(eng, lambda e: e.tensor_scalar(out, in0, s1, s2, op0, op1, **kw), r, w)

    def stt(self, eng, out, in0, scalar, in1, op0, op1, r=(), w=()):
        return self.op(eng, lambda e: e.scalar_tensor_tensor(out, in0, scalar, in1, op0, op1), r, w)

    def copy(self, eng, out, in_, r=(), w=()):
        if eng == "act":
            return self.op(eng, lambda e: e.copy(out, in_), r, w)
        return self.op(eng, lambda e: e.tensor_copy(out, in_), r, w)

    def memset(self, eng, ap, val, w=()):
        return self.op(eng, lambda e: e.memset(ap, val), (), w)


def _t5_bucket(d):
    n = np.maximum(d, 0)
    nf = np.maximum(n, 16).astype(np.float32)
    large = 16 + (np.log(nf / np.float32(16)) / np.float32(math.log(128 / 16)) * np.float32(16)).astype(np.int32)
    large = np.minimum(large, 31)
    return np.where(n < 16, n, large)


TW_B = 1024
TW_C = 640
TW_W = 1024
TAB = {
    "B": (TW_B + 127, 127),
    "C": (TW_C + 127, 127),
    "W": (TW_W + 127, 127),
    "M": (2048 + 16 * 126, 2047),
}
TABPAD = {k: ((v[0] + 511) // 512) * 512 for k, v in TAB.items()}


def _host_consts():
    c = {}
    c["ident"] = np.eye(128, dtype=np.float32)
    c["anti"] = np.ascontiguousarray(np.eye(128, dtype=np.float32)[::-1])
    c["anti127"] = np.ascontiguousarray(np.eye(127, dtype=np.float32)[::-1])
    k = np.arange(128)[:, None]
    q = np.arange(128)[None, :]
    c["tri"] = (q >= k).astype(np.float32)
    c["negtri"] = np.where(k >= q, 0.0, -1e30).astype(np.float32)
    qq = np.arange(128)[:, None]
    ss = np.arange(128)[None, :]
    c["negtri"] = np.where(ss <= qq, 0.0, -1e30).astype(np.float32)
    j = np.arange(512)[None, :]
    c["mask16"] = (((j - k) % 16) == 0).astype(np.float32)
    for ty, (n, off) in TAB.items():
        npad = TABPAD[ty]
        d = np.arange(npad) - off
        oh = np.zeros((32, npad), np.float32)
        oh[_t5_bucket(d), np.arange(npad)] = 1.0
        if ty == "B":
            cm = np.zeros(npad, np.float32)
            for r in (1, 4, 16):
                cm += ((d % r == 0) & (d >= 0) & (d <= 128 * r)).astype(np.float32)
        elif ty == "W":
            cm = ((d >= 0) & (d < 512)).astype(np.float32)
        else:
            cm = (d >= 0).astype(np.float32)
        cm[n:] = 0.0
        c["oh_" + ty] = oh
        c["cm_" + ty] = np.ascontiguousarray(np.broadcast_to(cm[None, :], (12, npad)))
    cs = np.arange(127)[:, None] * 16
    ssel = np.arange(32)[None, :] * 64
    ov = np.clip(np.minimum(cs + 32, ssel + 64) - np.maximum(cs, ssel), 0, None).astype(np.float32) / 32.0
    ovx = np.zeros((127, 33), np.float32)
    ovx[:, :32] = ov
    ovx[:, 32] = 1.0
    c["ovx"] = ovx
    ind = np.zeros((32, 16, 128), np.float32)
    for kt in range(16):
        for p in range(128):
            ind[2 * kt + p // 64, kt, p] = 1.0
    c["ind"] = ind.reshape(32, 16 * 128)
    t = np.arange(S)
    cur = t // 64
    jj = np.arange(32)[None, :]
    forced = (jj == 0) | (jj == cur[:, None]) | (jj == cur[:, None] - 1)
    adm = (jj * 64) <= t[:, None]
    fs = np.where(forced, 1e9, 0.0).astype(np.float32)
    c["selF"] = np.ascontiguousarray(fs.reshape(16, 128, 32).transpose(1, 0, 2)).reshape(128, 16 * 32)
    c["selA"] = np.ascontiguousarray(adm.astype(np.float32).reshape(16, 128, 32).transpose(1, 0, 2)).reshape(128, 16 * 32)
    c["selN"] = np.ascontiguousarray(np.where(adm, 0.0, -1e30).astype(np.float32).reshape(16, 128, 32).transpose(1, 0, 2)).reshape(128, 16 * 32)
    return c


CONST_SHAPES = None


def _const_shapes():
    global CONST_SHAPES
    if CONST_SHAPES is None:
        CONST_SHAPES = {k: v.shape for k, v in _host_consts().items()}
    return CONST_SHAPES


WEIGHT_SHAPES = {
    'w_in': (2, 1024, 2904), 'b_in': (2, 2904), 'a_conv': (2, 256, 4), 'a_norm': (2, 256),
    'd_cmp_pos': (2, 2, 64, 32), 'd_cmp_w1': (2, 2, 2048, 256), 'd_cmp_w2': (2, 2, 256, 64),
    'w_out': (2, 1024, 1024), 'b_out': (2, 128, 8), 'ln1_g': (2, 128, 8), 'ln1_b': (2, 128, 8),
    'w_ff1': (2, 1024, 4096), 'b_ff1': (2, 128, 32), 'w_ff2': (2, 4096, 1024), 'b_ff2': (2, 128, 8),
    'ln2_g': (2, 128, 8), 'ln2_b': (2, 128, 8), 'rel_bias': (32, 12),
}


def build_nc(n_seq=2, layers=(0, 1), debug=None, mixers="ABCD", ffn=True):
    nc = bass.Bass("TRN2", target_bir_lowering=False)
    dr = {}
    dr["x"] = nc.dram_tensor("x", [n_seq, S, D], F32, kind="ExternalInput").ap()
    for k, shp in WEIGHT_SHAPES.items():
        dr[k] = nc.dram_tensor(k, list(shp), F32, kind="ExternalInput").ap()
    for k, shp in _const_shapes().items():
        dr[k] = nc.dram_tensor("k_" + k, list(shp), F32, kind="ExternalInput").ap()
    out_d = nc.dram_tensor("out", [n_seq, S, D], F32, kind="ExternalOutput").ap()
    tv = {ty: nc.dram_tensor("tv_" + ty, [12, TABPAD[ty]], F32, kind="Internal").ap() for ty in TAB}
    tabB = nc.dram_tensor("tabB", [4, 128, TW_B], F32, kind="Internal").ap()
    tabC = nc.dram_tensor("tabC", [4, 128, TW_C], F32, kind="Internal").ap()
    tabS = nc.dram_tensor("tabS", [4, 128, TW_C], F32, kind="Internal").ap()
    tabW = nc.dram_tensor("tabW", [4, 128, TW_W], F32, kind="Internal").ap()
    tabM = nc.dram_tensor("tabM", [4, 128, 2048], BF16, kind="Internal").ap()
    scrA = nc.dram_tensor("scrA", [3, 4, S], F32, kind="Internal").ap()
    wf1b = nc.dram_tensor("wf1b", [2, 8, 128, 8 * 512], BF16, kind="Internal").ap()
    wf2b = nc.dram_tensor("wf2b", [2, 8, 128, 32 * 128], BF16, kind="Internal").ap()
    bgops = {}

    def dap(t, offset, pattern):
        return bass.AP(t.tensor, offset, pattern)

    KEY = lambda *a: a

    with ExitStack() as es:
        P = Prog(nc, es)

        uid = [0]

        def sb(stack, name, shape, dt=F32):
            uid[0] += 1
            return stack.enter_context(nc.sbuf_tensor("s%d_%s" % (uid[0], name), list(shape), dt))

        def psum(stack, name, shape, dt=F32):
            uid[0] += 1
            return stack.enter_context(nc.psum_tensor("p%d_%s" % (uid[0], name), list(shape), dt))

        xT = sb(es, "xT", [128, 8, S])
        ident_f = sb(es, "ident_f", [128, 128])
        ident_b = sb(es, "ident_b", [128, 128], BF16)
        ones_f = sb(es, "ones_f", [128, 128])
        rbrow = sb(es, "rbrow", [128, 12])
        P.dma(ident_f[:], dr["ident"], w=["ident_f"])
        P.dma(ident_b[:], dr["ident"], w=["ident_b"], eng="pool")
        P.memset("dve", ones_f[:], 1.0, w=["ones_f"])
        P.dma(rbrow[:], dap(dr["rel_bias"], 31 * 12, [[0, 128], [1, 12]]), w=["rbrow"])

        with ExitStack() as ph:
            rb = sb(ph, "rb", [32, 12])
            anti = sb(ph, "anti", [128, 128])
            anti127 = sb(ph, "anti127", [127, 127])
            oh = sb(ph, "oh", [32, 512])
            cm = sb(ph, "cm", [12, 512])
            vv = sb(ph, "vv", [12, 512])
            tps_ = [sb(ph, "tp%d" % i, [128, 2048]) for i in range(2)]
            tfs_ = [sb(ph, "tf%d" % i, [128, 2048]) for i in range(2)]
            psts = [pst_ for pst_ in (psum(ph, "pstb", [128, 512]),)]
            pst = psum(ph, "pst", [128, 512])
            P.dma(rb[:], dr["rel_bias"], w=["rb"])
            P.dma(anti[:], dr["anti"], w=["anti"])
            P.dma(anti127[:], dr["anti127"], w=["anti127"])
            for ty in TAB:
                npad = TABPAD[ty]
                for c0 in range(0, npad, 512):
                    P.dma(oh[:], dr["oh_" + ty][:, c0:c0 + 512], w=["oh"])
                    P.dma(cm[:], dr["cm_" + ty][:, c0:c0 + 512], w=["cm"])
                    P.matmul(pst[0:12, :], rb[:], oh[:], r=["rb", "oh"], w=["pst"])
                    P.activation(vv[:], pst[0:12, :], AF.Exp, r=["pst"], w=["vv"])
                    P.tt("dve", vv[:], vv[:], cm[:], ALU.mult, r=["vv", "cm"], w=["vv"])
                    P.dma(tv[ty][:, c0:c0 + 512], vv[:], r=["vv"], w=["tv" + ty])
            jobs = []
            for h in range(4):
                jobs.append(("B", h, tabB[h], TW_B, 1, 128))
                jobs.append(("C", 4 + h, tabC[h], TW_C, 1, 128))
                jobs.append(("C", 8 + h, tabS[h], TW_C, 1, 128))
                jobs.append(("W", 8 + h, tabW[h], TW_W, 1, 128))
                jobs.append(("M", 8 + h, tabM[h][0:127, :], 2048, 16, 127))
            nj = 0
            for ji, (ty, row, dst, W, pstep, npart) in enumerate(jobs):
                tp = tps_[ji % 2]
                tf = tfs_[ji % 2]
                tpk = KEY("tp", ji % 2)
                tfk = KEY("tf", ji % 2)
                src = dap(tv[ty], row * TABPAD[ty], [[pstep, npart], [1, W]])
                P.dma(tp[0:npart, 0:W], src, r=["tv" + ty], w=[tpk])
                fl = anti if npart == 128 else anti127
                for c0 in range(0, W, 512):
                    wd = min(512, W - c0)
                    pq = (pst, psts[0])[nj % 2]
                    pqk = KEY("pst", nj % 2)
                    nj += 1
                    P.matmul(pq[0:npart, 0:wd], fl[:], tp[0:npart, c0:c0 + wd], r=[tpk, "anti", "anti127"], w=[pqk])
                    P.copy("dve" if nj % 2 else "act", tf[0:npart, c0:c0 + wd], pq[0:npart, 0:wd], r=[pqk], w=[tfk])
                P.dma(dst, tf[0:npart, 0:W], r=[tfk], w=["tabs"], eng=("pool" if ty == "M" else "sp"))
            P.barrier()
            P.flush()

        for si in range(n_seq):
            for l in layers:
                _layer(nc, P, dr, out_d, si, l, layers, xT, ident_f, ident_b, ones_f, rbrow,
                       dict(tabB=tabB, tabC=tabC, tabS=tabS, tabW=tabW, tabM=tabM, scrA=scrA, wf1b=wf1b, wf2b=wf2b, bgops=bgops),
                       debug, mixers, ffn, sb, psum, dap)
        P.barrier(all_rings=True)
        P.flush()
    return nc


def _layer(nc, P, dr, out_d, si, l, layers, xT, ident_f, ident_b, ones_f, rbrow, scr,
           debug, mixers, ffn, sb, psum, dap):
    first_layer = (l == layers[0])
    last_layer = (l == layers[-1])
    KEY = lambda *a: a

    if first_layer:
        with ExitStack() as ph:
            xin = [sb(ph, "xin%d" % i, [128, D]) for i in range(2)]
            pt = [psum(ph, "ptx%d" % i, [128, 512]) for i in range(2)]
            n = 0
            for t in range(NT):
                xi = xin[t % 2]
                P.dma(xi[:], dr["x"][si, t * 128:(t + 1) * 128, :], w=[KEY("xin", t % 2)])
                for g in range(2):
                    p = pt[n % 2]
                    n += 1
                    for j in range(4):
                        kc = g * 4 + j
                        P.transpose(p[:, j * 128:(j + 1) * 128], xi[:, kc * 128:(kc + 1) * 128], ident_f[:],
                                    r=[KEY("xin", t % 2), "ident_f"], w=[KEY("ptx", (n - 1) % 2)])
                    eng = "act" if g == 0 else "dve"
                    P.copy(eng, xT[:, g * 4:(g + 1) * 4, t * 128:(t + 1) * 128],
                           p[:].rearrange("p (j t) -> p j t", j=4),
                           r=[KEY("ptx", (n - 1) % 2)], w=[KEY("xT", t)])
            P.barrier()
            P.flush()

    bgops = scr["bgops"]
    if ("f1", l, 0) not in bgops:
        for hg in range(8):
            bgops[("f1", l, hg)] = P.dma(scr["wf1b"][l, hg].rearrange("p (c n) -> p c n", c=8),
                                         dr["w_ff1"][l, :, hg * 512:(hg + 1) * 512].rearrange("(c p) n -> p c n", p=128),
                                         eng="pool", ring="bg")
        for fc in range(8):
            bgops[("f2", l, fc)] = P.dma(scr["wf2b"][l, fc].rearrange("p (c n) -> p c n", c=32),
                                         dr["w_ff2"][l, :, fc * 128:(fc + 1) * 128].rearrange("(c p) n -> p c n", p=128),
                                         eng="pool", ring="bg")

    with ExitStack() as mx:
        xbf = sb(mx, "xbf", [128, 8, S], BF16)
        mx2 = ExitStack()
        mixed = sb(mx2, "mixed", [128, NT, D], BF16)
        for kc in range(8):
            eng = ("act", "dve", "pool")[kc % 3]
            P.copy(eng, xbf[:, kc, :], xT[:, kc, :], w=["xbf"])
        if debug == "mixed":
            P.memset("pool", mixed[:], 0.0, w=["mixed0"])
        P.barrier()
        P.flush()

        bcol = lambda name, off, n: dap(dr["b_in"], l * D_IN + COL[name] + off, [[1, n], [1, 1]])
        brow = lambda name, off, n: dap(dr["b_in"], l * D_IN + COL[name] + off, [[0, 128], [1, n]])

        def load_w(dst, dcol, name, off, ncols, key):
            src = dr["w_in"][l, :, COL[name] + off:COL[name] + off + ncols].rearrange("(c p) n -> p c n", p=128)
            P.dma(dst[:, :, dcol:dcol + ncols], src, w=[key], eng="pool")

        def fm_proj(wt, wkey, M, psl, pskey, evac):
            for tb in range(4):
                ps = psl[tb % 2]
                for kc in range(8):
                    P.matmul(ps[0:M, :], wt[:, kc, 0:M], xbf[:, kc, tb * 512:(tb + 1) * 512],
                             start=(kc == 0), stop=(kc == 7), r=[wkey, "xbf"], w=[KEY(pskey, tb % 2)])
                evac(tb, ps, KEY(pskey, tb % 2))

        def tm_proj(wt, wkey, N, psl, pskey, evac):
            for t in range(NT):
                ps = psl[t % 2]
                for kc in range(8):
                    P.matmul(ps[:, 0:N], xbf[:, kc, t * 128:(t + 1) * 128], wt[:, kc, 0:N],
                             start=(kc == 0), stop=(kc == 7), r=[wkey, "xbf"], w=[KEY(pskey, t % 2)])
                evac(t, ps, KEY(pskey, t % 2))

        def attn_stream(psS, psO, lhs_k, rhs_q, v_rhs, kts_for_block, make_P, post, rkeys, blocks=range(4), prep=None):
            nS = [0]
            for b in blocks:
                kts = kts_for_block(b)
                ob = psO[b % 2][:, 0:260].rearrange("p (q d) -> p q d", q=4)
                okey = KEY("psO", id(psO[b % 2]))
                steps = []
                for kt in kts:
                    qs = max(512 * b, 128 * kt)
                    steps.append((kt, qs, 512 * (b + 1) - qs))
                NS = len(psS)

                def issue_S(i):
                    kt, qs, w = steps[i]
                    j = nS[0] % NS
                    nS[0] += 1
                    ss = psS[j]
                    skey = KEY("psS", id(ss))
                    P.matmul(ss[:, 0:w], lhs_k(kt), rhs_q(qs, w), r=rkeys, w=[skey])
                    aux = prep(b, kt, qs, w) if prep is not None else None
                    return ss, skey, aux
                q_ = []
                nxt = 0
                while nxt < len(steps) and len(q_) < NS - 1:
                    q_.append(issue_S(nxt))
                    nxt += 1
                firstmm = True
                for i, (kt, qs, w) in enumerate(steps):
                    if nxt < len(steps):
                        q_.append(issue_S(nxt))
                        nxt += 1
                    ss, skey, aux = q_.pop(0)
                    Pt, pkey = make_P(b, kt, qs, w, ss, skey) if aux is None else make_P(b, kt, qs, w, ss, skey, aux=aux)
                    for qi in range((qs - 512 * b) // 128, 4):
                        qt = 4 * b + qi
                        col = qt * 128 - qs
                        P.matmul(ob[:, qi, :], Pt[:, col:col + 128], v_rhs(kt),
                                 start=firstmm, stop=(kt == qt), r=[pkey] + rkeys, w=[okey], sgc=True)
                        firstmm = False
                    yield
                post(b, ob, okey)

        def run_streams(gens):
            gens = list(gens)
            while gens:
                for g in list(gens):
                    try:
                        next(g)
                    except StopIteration:
                        gens.remove(g)

        def dense_attn(*a, **kw):
            run_streams([attn_stream(*a, **kw)])

        if "A" in mixers:
            with ExitStack() as ph:
                psS = [psum(ph, "psSA%d" % i, [128, 512]) for i in range(4)]
                psP = psS[0:2]
                psO = [psum(ph, "psOA%d" % i, [128, 512]) for i in range(4)]
                qk = [sb(ph, "qkA%d" % i, [64, S], BF16) for i in range(4)]
                vaug = sb(ph, "vaugA", [128, NT, 4, 65], BF16)
                osig = sb(ph, "osigA", [128, NT, 256], BF16)
                gn = sb(ph, "gnA", [128, 256])
                tri = sb(ph, "triA", [128, 128], BF16)
                ab_tok = sb(ph, "ab_tok", [128, 4, NT])
                em_tok = sb(ph, "em_tok", [128, 4, NT])
                ph1 = ExitStack()
                ph1.__enter__()
                wt = [sb(ph1, "wA0", [128, 8, 256], BF16)] * 2
                pre = sb(ph1, "preA", [64, S + 3])
                cacc = sb(ph1, "caccA", [64, S])
                cw = sb(ph1, "cwA", [64, 4, 4])
                bq = sb(ph1, "bqA", [64, 4])
                bv = sb(ph1, "bvA", [128, 256])
                bo = sb(ph1, "boA", [128, 256])
                bg = sb(ph1, "bgA", [4, 2])
                gi = sb(ph1, "giA", [4, S])
                gf = sb(ph1, "gfA", [4, S])
                g2 = cacc[0:4, :]
                ones4 = ones_f[0:4, 0:1].to_broadcast([4, S])
                P.dma(tri[:], dr["tri"], w=["tri"], eng="pool")
                P.dma(gn[:], dap(dr["a_norm"], l * 256, [[0, 128], [1, 256]]), w=["gn"])
                P.memset("pool", vaug[:], 1.0, w=["vaug"])
                P.memset("pool", pre[:, 0:3], 0.0, w=["pre"])
                for i in range(4):
                    name = "a_q" if i < 2 else "a_k"
                    off = (i % 2) * 64
                    w_ = wt[i % 2]
                    load_w(w_, 0, name, off, 64, KEY("wA", 0))
                    P.dma(bq[:, i:i + 1], bcol(name, off, 64), w=["bq"])
                    cc = (0 if i < 2 else 128) + off
                    P.dma(cw[:, i, :], dr["a_conv"][l, cc:cc + 64, :], w=["cw"])

                    def evac(tb, ps, pk, i=i):
                        P.activation(pre[:, 3 + tb * 512:3 + (tb + 1) * 512], ps[0:64, :], AF.Identity,
                                     bias=bq[:, i:i + 1], r=[pk, "bq"], w=["pre"])
                    fm_proj(w_, KEY("wA", 0), 64, psP, "psP", evac)
                    P.ts("dve", cacc[:], pre[:, 3:3 + S], cw[:, i, 3:4], None, ALU.mult, r=["pre", "cw"], w=["cacc"])
                    for j in range(3):
                        P.stt("dve", cacc[:], pre[:, j:j + S], cw[:, i, j:j + 1], cacc[:], ALU.mult, ALU.add,
                              r=["pre", "cw", "cacc"], w=["cacc"])
                    P.activation(qk[i][:], cacc[:], AF.Silu, r=["cacc"], w=[KEY("qkA", i)])
                load_w(wt[0], 0, "a_v", 0, 256, KEY("wA", 0))
                P.dma(bv[:], brow("a_v", 0, 256), w=["bv"])

                def evac_v(t, ps, pk):
                    P.tt("dve", vaug[:, t, :, 0:64], ps[:, 0:256].rearrange("p (h d) -> p h d", h=4),
                         bv[:].rearrange("p (h d) -> p h d", h=4), ALU.add, r=[pk, "bv"], w=["vaug"])
                tm_proj(wt[0], KEY("wA", 0), 256, psP, "psP", evac_v)
                load_w(wt[1], 0, "a_o", 0, 256, KEY("wA", 0))
                P.dma(bo[:], brow("a_o", 0, 256), w=["bo"])

                def evac_o(t, ps, pk):
                    P.tt("dve", ps[:, 0:256], ps[:, 0:256], bo[:], ALU.add, r=[pk, "bo"], w=[pk])
                    P.activation(osig[:, t, :], ps[:, 0:256], AF.Sigmoid, r=[pk], w=["osig"])
                tm_proj(wt[1], KEY("wA", 0), 256, psP, "psP", evac_o)
                load_w(wt[0], 0, "a_i", 0, 4, KEY("wA", 0))
                load_w(wt[0], 4, "a_f", 0, 4, KEY("wA", 0))
                P.dma(bg[:, 0:1], bcol("a_i", 0, 4), w=["bg"])
                P.dma(bg[:, 1:2], bcol("a_f", 0, 4), w=["bg"])
                for gidx, gt in ((0, gi), (1, gf)):
                    for tb in range(4):
                        ps = psP[tb % 2]
                        for kc in range(8):
                            P.matmul(ps[0:4, :], wt[0][:, kc, 4 * gidx:4 * gidx + 4], xbf[:, kc, tb * 512:(tb + 1) * 512],
                                     start=(kc == 0), stop=(kc == 7), r=[KEY("wA", 0), "xbf"], w=[KEY("psP", tb % 2)])
                        P.activation(gt[:, tb * 512:(tb + 1) * 512], ps[0:4, :], AF.Identity, bias=bg[:, gidx:gidx + 1],
                                     r=[KEY("psP", tb % 2), "bg"], w=["g%d" % gidx])
                P.activation(gf[:], gf[:], AF.Exp, scale=-1.0, r=["g1"], w=["g1"])
                P.activation(gf[:], gf[:], AF.Ln, bias=1.0, r=["g1"], w=["g1"])
                P.op("dve", lambda e: e.tensor_tensor_scan(g2, ones4, gf[:], 0.0, ALU.mult, ALU.add),
                     r=["g1", "ones_f"], w=["cacc"])
                P.tt("dve", gi[:], gi[:], g2, ALU.add, r=["g0", "cacc"], w=["g0"])
                P.op("dve", lambda e: e.tensor_tensor_scan(gf[:], ones4, gi[:], 0.0, ALU.mult, ALU.max),
                     r=["g0", "ones_f", "g1"], w=["g1"])
                P.tt("dve", g2, gf[:], g2, ALU.subtract, r=["g1", "cacc"], w=["cacc"])
                P.activation(g2, g2, AF.Exp, scale=-1.0, r=["cacc"], w=["cacc"])
                P.ts("dve", gf[:], gf[:], -1.0, None, ALU.mult, r=["g1"], w=["g1"])
                P.ts("dve", gi[:], gi[:], math.log(32 ** -0.5), None, ALU.add, r=["g0"], w=["g0"])
                sA = scr["scrA"]
                P.dma(sA[0], gf[:], r=["g1"], w=["scrA0"])
                for src, dst, rk, wkk in ((gi, ab_tok, "g0", "ab_tok"), (g2, em_tok, "cacc", "em_tok")):
                    pv = psP[0][:, 0:64].rearrange("p (t h) -> p t h", h=4)
                    for t in range(NT):
                        P.transpose(pv[:, t, :], src[0:4, t * 128:(t + 1) * 128], ident_f[0:4, 0:4], r=[rk, "ident_f"], w=[KEY("psP", 0)])
                    P.copy("dve", dst[:].rearrange("p h t -> p t h"), pv, r=[KEY("psP", 0)], w=[wkk])
                P.barrier()
                P.flush()
                ph1.__exit__(None, None, None)
                numA = sb(ph, "numA", [128, NT, 4, 65])
                ph2 = ExitStack()
                negMs = [sb(ph2, "negMbc%d" % i, [128, S]) for i in range(2)]
                Wt = [sb(ph2, "WtA%d" % i, [128, 512]) for i in range(4)]
                Pb = [sb(ph2, "PbA%d" % i, [128, 512], BF16) for i in range(4)]
                for hp in range(2):
                    gens = []
                    for so_, h in enumerate((2 * hp, 2 * hp + 1)):
                        pair = h // 2
                        base = 32 * (h % 2)
                        negM = negMs[so_]
                        nkey = KEY("negM", so_)
                        P.dma(negM[:], dap(sA, h * S, [[0, 128], [1, S]]), w=[nkey])
                        cnt = [0]

                        def make_P(b, kt, qs, w, ss, skey, h=h, cnt=cnt, so=2 * so_, negM=negM, nkey=nkey):
                            i = so + cnt[0] % 2
                            cnt[0] += 1
                            P.activation(Wt[i][:, 0:w], negM[:, qs:qs + w], AF.Exp, bias=ab_tok[:, h, kt:kt + 1],
                                         r=[nkey, "ab_tok"], w=[KEY("Wt", i)])
                            P.tt("dve", Pb[i][:, 0:w], ss[:, 0:w], Wt[i][:, 0:w], ALU.mult, r=[skey, KEY("Wt", i)], w=[KEY("Pb", i)])
                            if kt >= 4 * b:
                                P.tt("pool", Pb[i][:, 0:128], Pb[i][:, 0:128], tri[:], ALU.mult, r=[KEY("Pb", i), "tri"], w=[KEY("Pb", i)])
                            return Pb[i], KEY("Pb", i)

                        def post(b, ob, okey, h=h):
                            P.copy("act", numA[:, 4 * b:4 * b + 4, h, :], ob[:, :, :], r=[okey], w=[KEY("numA", h)])

                        gens.append(attn_stream(psS[2 * so_:2 * so_ + 2], psO[2 * so_:2 * so_ + 2],
                                                lambda kt, pair=pair, base=base: qk[2 + pair][base:base + 32, kt * 128:(kt + 1) * 128],
                                                lambda qs, w, pair=pair, base=base: qk[pair][base:base + 32, qs:qs + w],
                                                lambda kt, h=h: vaug[:, kt, h, :],
                                                lambda b: list(range(0, 4 * b + 4)), make_P, post,
                                                [KEY("qkA", pair), KEY("qkA", 2 + pair), "vaug"]))
                    run_streams(gens)
                P.barrier()
                P.flush()
                ph2.close()
                with ExitStack() as pa:
                    G = NT * 4
                    hh = sb(pa, "hhA", [128, G, 64])
                    tq = sb(pa, "tqA", [128, G // 2, 64])
                    sA_ = sb(pa, "sA_", [128, 6, G])
                    num3 = numA[:].rearrange("p t h d -> p (t h) d")
                    den = num3[:, :, 64]
                    emv = em_tok[:].rearrange("p h t -> p t h")
                    a1 = sA_[:, 0, :]
                    a1v = a1.rearrange("p (t h) -> p t h", h=4)
                    P.ts("dve", a1, den, -1.0, None, ALU.mult, w=["a1"])
                    P.tt("dve", a1, a1, den, ALU.max, r=["a1"], w=["a1"])
                    P.tt("dve", a1v, a1v, emv, ALU.max, r=["a1"], w=["a1"])
                    P.op("dve", lambda e: e.reciprocal(sA_[:, 1, :], a1), r=["a1"], w=["rd"])
                    bc = lambda ap: ap.unsqueeze(2).to_broadcast([128, G, 64])
                    P.tt("dve", hh[:], num3[:, :, 0:64], bc(sA_[:, 1, :]), ALU.mult, r=["rd"], w=["hh"])
                    P.tt("dve", hh[:], hh[:], osig[:].rearrange("p t (h d) -> p (t h) d", h=4), ALU.mult, r=["hh"], w=["hh"])
                    P.op("dve", lambda e: e.tensor_reduce(sA_[:, 2, :], hh[:], AX.X, ALU.add), r=["hh"], w=["s1"])
                    P.ts("dve", sA_[:, 2, :], sA_[:, 2, :], 1.0 / 64, None, ALU.mult, r=["s1"], w=["s1"])
                    P.tt("dve", hh[:], hh[:], bc(sA_[:, 2, :]), ALU.subtract, r=["hh", "s1"], w=["hh"])
                    for hf in range(2):
                        gs = slice(hf * (G // 2), (hf + 1) * (G // 2))
                        P.tt("pool" if hf else "dve", tq[:], hh[:, gs, :], hh[:, gs, :], ALU.mult, r=["hh"], w=["tq"])
                        P.op("dve", lambda e, gs=gs: e.tensor_reduce(sA_[:, 3, gs], tq[:], AX.X, ALU.add), r=["tq"], w=["s2"])
                    P.activation(sA_[:, 4, :], sA_[:, 3, :], AF.Ln, bias=LN_EPS, scale=1.0 / 64, r=["s2"], w=["s4"])
                    P.activation(sA_[:, 4, :], sA_[:, 4, :], AF.Exp, scale=-0.5, r=["s4"], w=["s4"])
                    P.tt("dve", hh[:], hh[:], bc(sA_[:, 4, :]), ALU.mult, r=["hh", "s4"], w=["hh"])
                    P.tt("dve", mixed[:, :, 0:256], hh[:].rearrange("p (t h) d -> p t (h d)", h=4),
                         gn[:].unsqueeze(1).to_broadcast([128, NT, 256]), ALU.mult, r=["hh"], w=["mixedA"])
                    P.barrier()
                    P.flush()

        if "B" in mixers:
            with ExitStack() as ph:
                wt = [sb(ph, "wB%d" % i, [128, 8, 256], BF16) for i in range(2)]
                psS = [psum(ph, "psSB%d" % i, [128, 512]) for i in range(4)]
                psP = psS[0:2]
                psO = [psum(ph, "psOB%d" % i, [128, 512]) for i in range(4)]
                qT = [sb(ph, "qB%d" % i, [128, S], BF16) for i in range(2)]
                kT = [sb(ph, "kB%d" % i, [128, S], BF16) for i in range(2)]
                vaug = sb(ph, "vaugB", [128, NT, 4, 65], BF16)
                bq = sb(ph, "bqB", [128, 4])
                bv = sb(ph, "bvB", [128, 256])
                tab = sb(ph, "tabB", [128, 4, TW_B], BF16)
                m16 = sb(ph, "m16", [128, 512], BF16)
                Ef = [sb(ph, "EfB%d" % i, [128, 512], BF16) for i in range(4)]
                Eb = Ef
                Pb = [sb(ph, "PbB%d" % i, [128, 512], BF16) for i in range(4)]
                sms = [sb(ph, "smB%d" % i, [128, 4]) for i in range(2)]
                P.dma(m16[:], dr["mask16"], w=["m16"], eng="pool")
                for h in range(4):
                    P.dma(tab[:, h, :], scr["tabB"][h], w=["tab"], eng="pool")
                P.memset("pool", vaug[:], 1.0, w=["vaug"])
                for i in range(4):
                    name = "b_q" if i < 2 else "b_k"
                    off = (i % 2) * 128
                    load_w(wt[i % 2], 0, name, off, 128, KEY("wB", i % 2))
                    P.dma(bq[:, i:i + 1], bcol(name, off, 128), w=["bq"])
                    dst = qT[i] if i < 2 else kT[i - 2]

                    def evac(tb, ps, pk, i=i, dst=dst):
                        P.activation(dst[:, tb * 512:(tb + 1) * 512], ps[:, :], AF.Identity, bias=bq[:, i:i + 1],
                                     r=[pk, "bq"], w=[KEY("qkB", i)])
                    fm_proj(wt[i % 2], KEY("wB", i % 2), 128, psP, "psP", evac)
                load_w(wt[0], 0, "b_v", 0, 256, KEY("wB", 0))
                P.dma(bv[:], brow("b_v", 0, 256), w=["bv"])

                def evac_v(t, ps, pk):
                    P.tt("dve", vaug[:, t, :, 0:64], ps[:, 0:256].rearrange("p (h d) -> p h d", h=4),
                         bv[:].rearrange("p (h d) -> p h d", h=4), ALU.add, r=[pk, "bv"], w=["vaug"])
                tm_proj(wt[0], KEY("wB", 0), 256, psP, "psP", evac_v)
                P.barrier()
                P.flush()
                for hp in range(2):
                    gens = []
                    for so_, h in enumerate((2 * hp, 2 * hp + 1)):
                        pair = h // 2
                        base = 64 * (h % 2)
                        cnt = [0]
                        sm = sms[so_]
                        smk = KEY("sm", so_)

                        def make_P(b, kt, qs, w, ss, skey, h=h, cnt=cnt, so=2 * so_):
                            i = so + cnt[0] % 2
                            cnt[0] += 1
                            d0 = qs // 128 - kt
                            if kt >= 4 * b - 4:
                                P.activation(Ef[i][:, 0:w], ss[:, 0:w], AF.Exp, scale=0.125, r=[skey], w=[KEY("Ef", i)])
                                P.tt("dve", Pb[i][:, 0:w], Ef[i][:, 0:w], tab[:, h, 128 * d0:128 * d0 + w], ALU.mult,
                                     r=[KEY("Ef", i), "tab"], w=[KEY("Pb", i)])
                            else:
                                P.activation(Ef[i][:, 0:w], ss[:, 0:w], AF.Exp, scale=0.125, bias=rbrow[:, h:h + 1],
                                             r=[skey, "rbrow"], w=[KEY("Ef", i)])
                                P.tt("dve", Pb[i][:, 0:w], Ef[i][:, 0:w], m16[:, 0:w], ALU.mult,
                                     r=[KEY("Ef", i), "m16"], w=[KEY("Pb", i)])
                            return Pb[i], KEY("Pb", i)

                        def post(b, ob, okey, h=h, sm=sm, smk=smk):
                            P.op("dve", lambda e: e.reciprocal(sm[:, 0:4].unsqueeze(2), ob[:, :, 64:65]), r=[okey], w=[smk])
                            P.tt("dve", mixed[:, 4 * b:4 * b + 4, 256 + h * 64:256 + (h + 1) * 64], ob[:, :, 0:64],
                                 sm[:, 0:4].unsqueeze(2).to_broadcast([128, 4, 64]), ALU.mult, r=[okey, smk], w=[KEY("mixed", b, h)])

                        gens.append(attn_stream(psS[2 * so_:2 * so_ + 2], psO[2 * so_:2 * so_ + 2],
                                                lambda kt, pair=pair, base=base: kT[pair][base:base + 64, kt * 128:(kt + 1) * 128],
                                                lambda qs, w, pair=pair, base=base: qT[pair][base:base + 64, qs:qs + w],
                                                lambda kt, h=h: vaug[:, kt, h, :],
                                                lambda b: list(range(0, 4 * b + 4)), make_P, post,
                                                [KEY("qkB", pair), KEY("qkB", 2 + pair), "vaug"]))
                    run_streams(gens)
                P.barrier()
                P.flush()

        if "C" in mixers:
            _mixer_c(nc, P, dr, l, scr, sb, psum, dap, xbf, mixed, rbrow, ident_b, load_w, bcol, brow,
                     fm_proj, tm_proj, dense_attn, KEY)
        if "D" in mixers:
            _mixer_d(nc, P, dr, l, scr, sb, psum, dap, xbf, mixed, rbrow, ident_b, load_w, bcol, brow,
                     fm_proj, tm_proj, dense_attn, KEY)

        if debug == "mixed":
            for t in range(NT):
                P.dma(out_d[si, t * 128:(t + 1) * 128, :], mixed[:, t, :], r=[KEY("mixed", t), "mixed0"], w=["out"], eng="pool")
            P.barrier()
            P.flush()
            mx2.close()
            return

        with ExitStack() as ph:
            ptb = [psum(ph, "ptb%d" % i, [128, 8, 128], BF16)[:, 0:4, :] for i in range(2)]
            n = 0
            for t in range(NT):
                for g in range(2):
                    p = ptb[n % 2]
                    pk = KEY("ptb", n % 2)
                    n += 1
                    for j in range(4):
                        kc = g * 4 + j
                        P.transpose(p[:, j, :], mixed[:, t, kc * 128:(kc + 1) * 128], ident_b[:], r=["ident_b"], w=[pk])
                    P.copy("act" if g == 0 else "dve", xbf[:, g * 4:(g + 1) * 4, t * 128:(t + 1) * 128], p[:], r=[pk], w=["xbfm"])
            P.barrier()
            P.flush()
        mx2.close()
        _dense_tail(nc, P, dr, out_d, si, l, last_layer, xT, xbf, ident_f, ones_f, sb, psum, dap, KEY, ffn, debug, scr=scr)


def _ln_fm(nc, P, xT, ones_f, gcol, bcol_, tmp, st, psl, KEY, xbf_out=None):
    for tb in range(4):
        sl = slice(tb * 512, (tb + 1) * 512)
        p1, p2 = psl
        for kc in range(8):
            P.matmul(p1[:, :], ones_f[:], xT[:, kc, sl], start=(kc == 0), stop=(kc == 7), r=["xTw", "ones_f"], w=["lnp1"])
        for kc in range(8):
            P.activation(tmp[kc % 2][:], xT[:, kc, sl], AF.Square, r=["xTw"], w=[KEY("lnsq", kc % 2)])
            P.matmul(p2[:, :], ones_f[:], tmp[kc % 2][:], start=(kc == 0), stop=(kc == 7), r=[KEY("lnsq", kc % 2), "ones_f"], w=["lnp2"])
        mean, rstd = st
        P.activation(mean[:], p1[:, :], AF.Identity, scale=1.0 / D, r=["lnp1"], w=["mean"])
        P.activation(rstd[:], p2[:, :], AF.Identity, scale=1.0 / D, r=["lnp2"], w=["rstd"])
        P.tt("dve", tmp[0][:], mean[:], mean[:], ALU.mult, r=["mean", KEY("lnsq", 0)], w=[KEY("lnsq", 0)])
        P.tt("dve", rstd[:], rstd[:], tmp[0][:], ALU.subtract, r=["rstd", KEY("lnsq", 0)], w=["rstd"])
        P.activation(rstd[:], rstd[:], AF.Sqrt, bias=LN_EPS, r=["rstd"], w=["rstd"])
        P.op("dve", lambda e: e.reciprocal(rstd[:], rstd[:]), r=["rstd"], w=["rstd"])
        for kc in range(8):
            P.tt("dve", xT[:, kc, sl], xT[:, kc, sl], mean[:], ALU.subtract, r=["xTw", "mean"], w=["xTw"])
            P.tt("pool", xT[:, kc, sl], xT[:, kc, sl], rstd[:], ALU.mult, r=["xTw", "rstd"], w=["xTw"])
            P.ts("dve", xT[:, kc, sl], xT[:, kc, sl], gcol[:, kc:kc + 1], bcol_[:, kc:kc + 1], ALU.mult, ALU.add,
                 r=["xTw", "lng"], w=["xTw"])
            if xbf_out is not None:
                P.copy("act", xbf_out[:, kc, sl], xT[:, kc, sl], r=["xTw"], w=["xbfw"])


def _dense_tail(nc, P, dr, out_d, si, l, last_layer, xT, xbf, ident_f, ones_f, sb, psum, dap, KEY, ffn, debug, scr=None):
    with ExitStack() as ph:
        wo = [sb(ph, "wo%d" % i, [128, 8, 256], BF16) for i in range(2)]
        ps = [psum(ph, "pso%d" % i, [128, 512]) for i in range(4)]
        tmpf = [sb(ph, "tmpo%d" % i, [128, 512]) for i in range(2)]
        bo = sb(ph, "bout", [128, 8])
        g1 = sb(ph, "ln1g", [128, 8])
        b1 = sb(ph, "ln1b", [128, 8])
        mean = sb(ph, "mean", [128, 512])
        rstd = sb(ph, "rstd", [128, 512])
        P.dma(bo[:], dr["b_out"][l], w=["bo"])
        P.dma(g1[:], dr["ln1_g"][l], w=["lng"])
        P.dma(b1[:], dr["ln1_b"][l], w=["lng"])
        n = 0
        for fg in range(4):
            w_ = wo[fg % 2]
            wk = KEY("wo", fg % 2)
            P.dma(w_[:], dr["w_out"][l, :, fg * 256:(fg + 1) * 256].rearrange("(c p) n -> p c n", p=128), w=[wk], eng="pool")
            for f2 in range(2):
                fc = fg * 2 + f2
                for tb in range(4):
                    p = ps[n % 4]
                    pk = KEY("pso", n % 4)
                    tf_ = tmpf[n % 2]
                    tk = KEY("tmpo", n % 2)
                    n += 1
                    sl = slice(tb * 512, (tb + 1) * 512)
                    for kc in range(8):
                        P.matmul(p[:, :], w_[:, kc, f2 * 128:(f2 + 1) * 128], xbf[:, kc, sl], start=(kc == 0), stop=(kc == 7),
                                 r=[wk, "xbfm"], w=[pk])
                    P.activation(tf_[:], p[:, :], AF.Identity, bias=bo[:, fc:fc + 1], r=[pk, "bo"], w=[tk])
                    P.stt("dve", xT[:, fc, sl], xT[:, fc, sl], ALPHA, tf_[:], ALU.mult, ALU.add, r=[tk, "xTw"], w=["xTw"])
        _ln_fm(nc, P, xT, ones_f, g1, b1, tmpf, (mean, rstd), (ps[0], ps[1]), KEY, xbf_out=xbf)
        P.barrier()
        P.flush()
    if debug == "x1":
        _store_out(nc, P, out_d, si, xT, ident_f, sb, psum, KEY)
        return
    if ffn:
        with ExitStack() as ph:
            hT = sb(ph, "hT", [128, 32, 1024], BF16)
            w1 = [sb(ph, "w1_%d" % i, [128, 8, 512], BF16) for i in range(2)]
            w2 = [sb(ph, "w2_%d" % i, [128, 32, 128], BF16) for i in range(2)]
            ps = [psum(ph, "psf%d" % i, [128, 512]) for i in range(4)]
            tmpf = [sb(ph, "tmpf%d" % i, [128, 512]) for i in range(2)]
            b1 = sb(ph, "bff1", [128, 32])
            b2 = sb(ph, "bff2", [128, 8])
            g2 = sb(ph, "ln2g", [128, 8])
            bb2 = sb(ph, "ln2b", [128, 8])
            mean = sb(ph, "mean2", [128, 512])
            rstd = sb(ph, "rstd2", [128, 512])
            P.dma(b1[:], dr["b_ff1"][l], w=["b1"])
            P.dma(b2[:], dr["b_ff2"][l], w=["b2"])
            P.dma(g2[:], dr["ln2_g"][l], w=["lng"])
            P.dma(bb2[:], dr["ln2_b"][l], w=["lng"])
            n = 0
            for half in range(2):
                t0 = half * 1024
                for hg in range(8):
                    w_ = w1[hg % 2]
                    wk = KEY("w1", hg % 2)
                    P.dma(w_[:], scr["wf1b"][l, hg].rearrange("p (c n) -> p c n", c=8), w=[wk], after=[scr["bgops"][("f1", l, hg)]])
                    for h4 in range(4):
                        hc = hg * 4 + h4
                        for tb in range(2):
                            p = ps[n % 4]
                            pk = KEY("psf", n % 4)
                            tf_ = tmpf[n % 2]
                            tk = KEY("tmpf", n % 2)
                            n += 1
                            sl = slice(t0 + tb * 512, t0 + (tb + 1) * 512)
                            for kc in range(8):
                                P.matmul(p[:, :], w_[:, kc, h4 * 128:(h4 + 1) * 128], xbf[:, kc, sl], start=(kc == 0), stop=(kc == 7),
                                         r=[wk, "xbfw"], w=[pk])
                            P.activation(tf_[:], p[:, :], AF.Relu, bias=b1[:, hc:hc + 1], r=[pk, "b1"], w=[tk])
                            P.tt("pool", hT[:, hc, tb * 512:(tb + 1) * 512], tf_[:], tf_[:], ALU.mult, r=[tk], w=[KEY("hT", hc)])
                for fc in range(8):
                    w_ = w2[fc % 2]
                    wk = KEY("w2", fc % 2)
                    P.dma(w_[:], scr["wf2b"][l, fc].rearrange("p (c n) -> p c n", c=32), w=[wk], after=[scr["bgops"][("f2", l, fc)]])
                    for tb in range(2):
                        p = ps[n % 4]
                        pk = KEY("psf", n % 4)
                        tf_ = tmpf[n % 2]
                        tk = KEY("tmpf", n % 2)
                        n += 1
                        sl = slice(t0 + tb * 512, t0 + (tb + 1) * 512)
                        for kc in range(32):
                            P.matmul(p[:, :], w_[:, kc, :], hT[:, kc, tb * 512:(tb + 1) * 512], start=(kc == 0), stop=(kc == 31),
                                     r=[wk, KEY("hT", kc)], w=[pk])
                        P.activation(tf_[:], p[:, :], AF.Identity, bias=b2[:, fc:fc + 1], r=[pk, "b2"], w=[tk])
                        P.stt("dve", xT[:, fc, sl], xT[:, fc, sl], ALPHA, tf_[:], ALU.mult, ALU.add, r=[tk, "xTw"], w=["xTw"])
            _ln_fm(nc, P, xT, ones_f, g2, bb2, tmpf, (mean, rstd), (ps[0], ps[1]), KEY)
            P.barrier()
            P.flush()
    if last_layer:
        _store_out(nc, P, out_d, si, xT, ident_f, sb, psum, KEY)


def _store_out(nc, P, out_d, si, xT, ident_f, sb, psum, KEY):
    with ExitStack() as ph:
        xo = [sb(ph, "xo%d" % i, [128, 1024]) for i in range(2)]
        pt = [psum(ph, "pto%d" % i, [128, 512]) for i in range(2)]
        n = 0
        for t in range(NT):
            x_ = xo[t % 2]
            xk = KEY("xo", t % 2)
            for g in range(2):
                p = pt[n % 2]
                pk = KEY("pto", n % 2)
                n += 1
                for j in range(4):
                    kc = g * 4 + j
                    P.transpose(p[:, j * 128:(j + 1) * 128], xT[:, kc, t * 128:(t + 1) * 128], ident_f[:], r=["ident_f"], w=[pk])
                P.copy("act" if g == 0 else "dve", x_[:, g * 512:(g + 1) * 512], p[:, :], r=[pk], w=[xk])
            P.dma(out_d[si, t * 128:(t + 1) * 128, :], x_[:], r=[xk], w=["out"])
        P.barrier()
        P.flush()


def _mixer_c(nc, P, dr, l, scr, sb, psum, dap, xbf, mixed, rbrow, ident_b, load_w, bcol, brow,
             fm_proj, tm_proj, dense_attn, KEY):
    with ExitStack() as ph:
        psP = [psum(ph, "psPC%d" % i, [128, 512]) for i in range(2)]
        psS = [psum(ph, "psSC%d" % i, [128, 512]) for i in range(3)]
        psO = [psum(ph, "psOC%d" % i, [128, 512]) for i in range(2)]
        psT = psum(ph, "psTC", [128, 8, 128], BF16)
        qT = [sb(ph, "qC%d" % i, [128, S], BF16) for i in range(2)]
        iqT = [sb(ph, "iqC%d" % i, [128, S], BF16) for i in range(2)]
        kT = sb(ph, "kC", [128, S], BF16)
        ikT = sb(ph, "ikC", [128, S], BF16)
        vaug = sb(ph, "vaugC", [128, NT, 65], BF16)
        iw = sb(ph, "iwC", [128, NT, 4])
        pj = ExitStack()
        wt = [sb(pj, "wC0", [128, 8, 128], BF16)] * 2
        bq = sb(pj, "bqC", [128, 6])
        bv = sb(pj, "bvC", [128, 68])
        P.memset("pool", vaug[:], 1.0, w=["vaug"])
        jobs = [("c_q", 0, 128, qT[0], False), ("c_q", 128, 128, qT[1], False),
                ("c_iq", 0, 128, iqT[0], False), ("c_iq", 128, 128, iqT[1], False),
                ("c_k", 0, 64, kT, True), ("c_ik", 0, 64, ikT, True)]
        for i, (name, off, ncol, dst, dup) in enumerate(jobs):
            w_ = wt[i % 2]
            wk = KEY("wC", 0)
            load_w(w_, 0, name, off, ncol, wk)
            P.dma(bq[0:ncol, i:i + 1], bcol(name, off, ncol), w=["bq"])
            if dup:
                load_w(w_, 64, name, off, ncol, wk)
                P.dma(bq[64:128, i:i + 1], bcol(name, off, ncol), w=["bq"])

            def evac(tb, ps, pk, i=i, dst=dst):
                P.activation(dst[:, tb * 512:(tb + 1) * 512], ps[:, :], AF.Identity, bias=bq[:, i:i + 1],
                             r=[pk, "bq"], w=[KEY("fmC", i)])
            fm_proj(w_, wk, 128, psP, "psP", evac)
        load_w(wt[0], 0, "c_v", 0, 64, KEY("wC", 0))
        load_w(wt[0], 64, "c_iw", 0, 4, KEY("wC", 0))
        P.dma(bv[:, 0:64], brow("c_v", 0, 64), w=["bv"])
        P.dma(bv[:, 64:68], brow("c_iw", 0, 4), w=["bv"])

        def evac_v(t, ps, pk):
            P.tt("dve", vaug[:, t, 0:64], ps[:, 0:64], bv[:, 0:64], ALU.add, r=[pk, "bv"], w=["vaug"])
            P.tt("dve", iw[:, t, :], ps[:, 64:68], bv[:, 64:68], ALU.add, r=[pk, "bv"], w=["iw"])
        tm_proj(wt[0], KEY("wC", 0), 68, psP, "psP", evac_v)
        P.barrier()
        P.flush()
        pj.close()
        tab = sb(ph, "tabCs", [128, 4, TW_C], BF16)
        negtri = sb(ph, "negtriC", [128, 128])
        scs = [sb(ph, "scC%d" % i, [128, S]) for i in range(2)]
        junkD = sb(ph, "junkDC", [128, 832], BF16)
        sel = sb(ph, "selC", [128, S], BF16)
        maskT = sb(ph, "maskTC", [128, NT, 512], BF16)
        Ef = [sb(ph, "EfC%d" % i, [128, 512], BF16) for i in range(3)]
        Pb = [sb(ph, "PbC%d" % i, [128, 512], BF16) for i in range(3)]
        sm = sb(ph, "smC", [128, 8])
        bs = sb(ph, "bsC", [128, 8])
        NIT = 14
        sg = sb(ph, "sgC", [128, 24])
        junkA = sb(ph, "junkAC", [128, 1232], BF16)
        dk = sb(ph, "dkC", [128, 24])
        cn = sb(ph, "cnC", [128, 24])
        md = sb(ph, "mdC", [128, 24])
        p2row = sb(ph, "p2rowC", [128, 24])
        for k_ in range(NIT + 1):
            P.memset("pool", p2row[:, k_:k_ + 1], 2.0 ** -(k_ + 1), w=["p2row"])
        P.dma(negtri[:], dr["negtri"], w=["negtri"])
        for h in range(4):
            P.dma(tab[:, h, :], scr["tabC"][h], w=["tab"], eng="pool")
        P.memset("pool", maskT[:], 1.0, w=["maskT"])
        fmkeys = [KEY("fmC", i) for i in range(6)]
        npp = [0]

        def index_steps(qt):
            nk = 128 * (qt + 1)
            sc = scs[qt % 2]
            sck = KEY("sc", qt % 2)
            steps = []
            for s0 in range(0, nk, 512):
                wd = min(512, nk - s0)
                for hi in range(4):
                    def step(s0=s0, wd=wd, hi=hi):
                        pair = hi // 2
                        base = 64 * (hi % 2)
                        i2 = npp[0] % 2
                        npp[0] += 1
                        ps = psP[i2]
                        pk = KEY("psP", i2)
                        P.matmul(ps[:, 0:wd], iqT[pair][base:base + 64, qt * 128:(qt + 1) * 128], ikT[base:base + 64, s0:s0 + wd],
                                 r=fmkeys, w=[pk])
                        P.activation(ps[:, 0:wd], ps[:, 0:wd], AF.Relu, r=[pk], w=[pk])
                        if hi == 0:
                            P.ts("dve", sc[:, s0:s0 + wd], ps[:, 0:wd], iw[:, qt, 0:1], None, ALU.mult,
                                 r=[pk, "iw"], w=[sck])
                        else:
                            P.stt("dve", sc[:, s0:s0 + wd], ps[:, 0:wd], iw[:, qt, hi:hi + 1], sc[:, s0:s0 + wd], ALU.mult, ALU.add,
                                  r=[pk, "iw", sck], w=[sck])
                    steps.append(step)
            return steps

        tiles = list(range(2, NT))
        for st_ in index_steps(tiles[0]):
            st_()
        for b in range(4):
            for qi in range(4):
                qt = 4 * b + qi
                if qt < 2:
                    continue
                nk = 128 * (qt + 1)
                sc = scs[qt % 2]
                sck = KEY("sc", qt % 2)
                nxt = index_steps(qt + 1) if qt + 1 < NT else []
                per = -(-len(nxt) // NIT)
                P.op("dve", lambda e, nk=nk, sc=sc: e.tensor_reduce(bs[:, 0:1], sc[:, 0:nk], AX.X, ALU.min), r=[sck], w=["lo"])
                P.op("dve", lambda e, nk=nk, sc=sc: e.tensor_reduce(bs[:, 1:2], sc[:, 0:nk], AX.X, ALU.max), r=[sck], w=["hi"])
                P.tt("dve", bs[:, 2:3], bs[:, 1:2], bs[:, 0:1], ALU.subtract, r=["lo", "hi"], w=["d"])
                P.tt("dve", sc[:, qt * 128:nk], sc[:, qt * 128:nk], negtri[:], ALU.add, r=[sck, "negtri"], w=[sck])
                P.ts("dve", dk[:, 0:NIT + 1], p2row[:, 0:NIT + 1], bs[:, 2:3], None, ALU.mult, r=["d", "p2row"], w=["dk"])
                P.memset("dve", cn[:, 0:NIT], 0.0, w=["cn"])
                P.tt("dve", md[:, 0:1], bs[:, 0:1], dk[:, 0:1], ALU.add, r=["lo", "dk"], w=["md"])
                nd = (int(nk * 0.40) // 16) * 16
                na = nk - nd
                P.memset("dve", sg[:, 0:NIT], 0.0, w=["sg"])
                for it in range(NIT):
                    P.activation(junkA[:, 0:na], sc[:, nd:nk], AF.Sign, bias=md[:, it:it + 1], scale=-1.0,
                                 r=[sck, "md", "sg"], w=["junkA", "sg"], accum_out=sg[:, it:it + 1])
                    P.ts("dve", junkD[:, 0:nd], sc[:, 0:nd], md[:, it:it + 1], None, ALU.is_ge, ALU.add, r=[sck, "md", "cn"], w=["junkD", "cn"],
                         accum_out=cn[:, it:it + 1])
                    P.stt("dve", bs[:, 6:7], sg[:, it:it + 1], -0.5, cn[:, it:it + 1], ALU.mult, ALU.add, r=["sg", "cn"], w=["tc"])
                    P.ts("dve", bs[:, 5:6], bs[:, 6:7], 255.5 - na / 2.0, 0.5, ALU.is_ge, ALU.subtract, r=["tc"], w=["ge"])
                    P.stt("dve", md[:, it + 1:it + 2], bs[:, 5:6], dk[:, it:it + 1], md[:, it:it + 1], ALU.mult, ALU.add,
                          r=["ge", "dk", "md"], w=["md"])
                    for _ in range(per):
                        if nxt:
                            nxt.pop(0)()
                while nxt:
                    nxt.pop(0)()
                P.tt("dve", bs[:, 0:1], md[:, NIT:NIT + 1], dk[:, NIT:NIT + 1], ALU.subtract, r=["md", "dk"], w=["lo"])
                P.ts("dve", sel[:, 0:nk], sc[:, 0:nk], bs[:, 0:1], None, ALU.is_ge, r=[sck, "lo"], w=["sel"])
                for k0 in range(0, qt + 1, 8):
                    kn = min(8, qt + 1 - k0)
                    for j in range(kn):
                        P.transpose(psT[:, j, :], sel[:, (k0 + j) * 128:(k0 + j + 1) * 128], ident_b[:], r=["sel", "ident_b"], w=["psT"])
                    P.copy("act", maskT[:, k0:k0 + kn, qi * 128:(qi + 1) * 128], psT[:, 0:kn, :], r=["psT"], w=["maskT"])
            for h in range(4):
                pair = h // 2
                base = 64 * (h % 2)
                cnt = [0]

                def make_P(b_, kt, qs, w, ss, skey, h=h, cnt=cnt):
                    i = cnt[0] % 3
                    cnt[0] += 1
                    d0 = qs // 128 - kt
                    mk = maskT[:, kt, qs - 512 * b_:qs - 512 * b_ + w]
                    if kt >= 4 * b_ - 1:
                        P.activation(Ef[i][:, 0:w], ss[:, 0:w], AF.Exp, scale=0.125, r=[skey], w=[KEY("Ef", i)])
                        P.tt("dve", Ef[i][:, 0:w], Ef[i][:, 0:w], tab[:, h, 128 * d0:128 * d0 + w], ALU.mult,
                             r=[KEY("Ef", i), "tab"], w=[KEY("Ef", i)])
                        P.tt("dve", Pb[i][:, 0:w], Ef[i][:, 0:w], mk, ALU.mult, r=[KEY("Ef", i), "maskT"], w=[KEY("Pb", i)])
                    else:
                        P.activation(Ef[i][:, 0:w], ss[:, 0:w], AF.Exp, scale=0.125, bias=rbrow[:, 4 + h:5 + h],
                                     r=[skey, "rbrow"], w=[KEY("Ef", i)])
                        P.tt("dve", Pb[i][:, 0:w], Ef[i][:, 0:w], mk, ALU.mult, r=[KEY("Ef", i), "maskT"], w=[KEY("Pb", i)])
                    return Pb[i], KEY("Pb", i)

                def post(b_, ob, okey, h=h):
                    P.op("dve", lambda e: e.reciprocal(sm[:, 0:4].unsqueeze(2), ob[:, :, 64:65]), r=[okey], w=["sm0"])
                    P.tt("dve", mixed[:, 4 * b_:4 * b_ + 4, 512 + h * 64:512 + (h + 1) * 64], ob[:, :, 0:64],
                         sm[:, 0:4].unsqueeze(2).to_broadcast([128, 4, 64]), ALU.mult, r=[okey, "sm0"], w=[KEY("mixed", b_)])

                dense_attn(psS, psO,
                           lambda kt: kT[base:base + 64, kt * 128:(kt + 1) * 128],
                           lambda qs, w: qT[pair][base:base + 64, qs:qs + w],
                           lambda kt: vaug[:, kt, :],
                           lambda b_: list(range(0, 4 * b_ + 4)), make_P, post,
                           fmkeys + ["vaug"], blocks=[b])
        P.barrier()
        P.flush()


def _mixer_d(nc, P, dr, l, scr, sb, psum, dap, xbf, mixed, rbrow, ident_b, load_w, bcol, brow,
             fm_proj, tm_proj, dense_attn, KEY):
    with ExitStack() as ph:
        psP = [psum(ph, "psPD%d" % i, [128, 512]) for i in range(2)]
        psS = [psum(ph, "psSD%d" % i, [128, 512]) for i in range(2)]
        psO = [psum(ph, "psOD%d" % i, [128, 512]) for i in range(2)]
        psM = psum(ph, "psMD", [128, 512])
        qT = [sb(ph, "qD%d" % i, [128, S], BF16) for i in range(2)]
        ksT = sb(ph, "ksD", [128, S], BF16)
        kwT = sb(ph, "kwD", [128, S], BF16)
        vS = sb(ph, "vSD", [128, NT, 65], BF16)
        vW = sb(ph, "vWD", [128, NT, 65], BF16)
        gate = sb(ph, "gD", [128, NT, 12])
        kcmp = sb(ph, "kcmpD", [128, 128], BF16)
        vcmp = sb(ph, "vcmpD", [128, 65], BF16)
        acc = sb(ph, "accD", [128, NT, 256])
        selT = sb(ph, "selTD", [32, S], BF16)
        Ef = [sb(ph, "EfD%d" % i, [128, 512], BF16) for i in range(3)]
        Eb = [sb(ph, "EbD%d" % i, [128, 512], BF16) for i in range(3)]
        Pb = [sb(ph, "PbD%d" % i, [128, 512], BF16) for i in range(3)]
        sm = sb(ph, "smD", [128, 12])
        tmpD = sb(ph, "tmpD", [128, 4, 64])
        P.memset("pool", vS[:], 1.0, w=["vS"])
        P.memset("pool", vW[:], 1.0, w=["vW"])
        P.memset("pool", vcmp[:], 1.0, w=["vcmp"])
        with ExitStack() as p1:
            wt = [sb(p1, "wD%d" % i, [128, 8, 140], BF16) for i in range(2)]
            kvc = sb(p1, "kvcD", [128, S], BF16)
            bq = sb(p1, "bqD", [128, 6])
            bv = sb(p1, "bvD", [128, 140])
            W1 = sb(p1, "W1D", [128, 32, 256], BF16)
            w2k = sb(p1, "w2kD", [128, 2, 128], BF16)
            w2v = sb(p1, "w2vD", [128, 2, 64], BF16)
            posT = sb(p1, "posTD", [128, 32], BF16)
            hs = sb(p1, "hsD", [128, 2, 2, 128], BF16)
            hb = sb(p1, "hbD", [128, 4])
            psX = psum(p1, "psXD", [128, 512])
            jobs = [("d_q", 0, 128, qT[0], 0), ("d_q", 128, 128, qT[1], 0),
                    ("d_kc", 0, 128, kvc, 0), ("d_ks", 0, 64, ksT, 1), ("d_kw", 0, 64, kwT, 1)]
            for i, (name, off, ncol, dst, dup) in enumerate(jobs):
                w_ = wt[i % 2]
                wk = KEY("wD", i % 2)
                load_w(w_, 0, name, off, ncol, wk)
                P.dma(bq[0:ncol, i:i + 1], bcol(name, off, ncol), w=["bq"])
                if dup:
                    load_w(w_, 64, name, off, ncol, wk)
                    P.dma(bq[64:128, i:i + 1], bcol(name, off, ncol), w=["bq"])

                def evac(tb, ps, pk, i=i, dst=dst):
                    P.activation(dst[:, tb * 512:(tb + 1) * 512], ps[:, :], AF.Identity, bias=bq[:, i:i + 1],
                                 r=[pk, "bq"], w=[KEY("fmD", i)])
                fm_proj(w_, wk, 128, psP, "psP", evac)
            load_w(wt[0], 0, "d_vs", 0, 64, KEY("wD", 0))
            load_w(wt[0], 64, "d_vw", 0, 64, KEY("wD", 0))
            load_w(wt[0], 128, "d_g", 0, 12, KEY("wD", 0))
            P.dma(bv[:, 0:64], brow("d_vs", 0, 64), w=["bv"])
            P.dma(bv[:, 64:128], brow("d_vw", 0, 64), w=["bv"])
            P.dma(bv[:, 128:140], brow("d_g", 0, 12), w=["bv"])

            def evac_v(t, ps, pk):
                P.tt("dve", vS[:, t, 0:64], ps[:, 0:64], bv[:, 0:64], ALU.add, r=[pk, "bv"], w=["vS"])
                P.tt("dve", vW[:, t, 0:64], ps[:, 64:128], bv[:, 64:128], ALU.add, r=[pk, "bv"], w=["vW"])
                P.tt("dve", ps[:, 128:140], ps[:, 128:140], bv[:, 128:140], ALU.add, r=[pk, "bv"], w=[pk])
                P.activation(gate[:, t, :], ps[:, 128:140], AF.Sigmoid, r=[pk], w=["gate"])
            tm_proj(wt[0], KEY("wD", 0), 140, psP, "psP", evac_v)
            for wh in range(2):
                P.dma(W1[64 * wh:64 * wh + 64, :, :], dr["d_cmp_w1"][l, wh].rearrange("(l d) j -> d l j", d=64), w=["W1"], eng="pool")
                P.dma(posT[64 * wh:64 * wh + 64, :], dr["d_cmp_pos"][l, wh], w=["posT"], eng="pool")
            for c2 in range(2):
                P.dma(w2k[:, :, 64 * c2:64 * c2 + 64], dr["d_cmp_w2"][l, 0].rearrange("(h p) n -> p h n", p=128), w=["w2k"], eng="pool")
            P.dma(w2v[:, :, :], dr["d_cmp_w2"][l, 1].rearrange("(h p) n -> p h n", p=128), w=["w2v"], eng="pool")
            kv3 = kvc[:].rearrange("p (c s) -> p c s", s=16)
            for wh in range(2):
                base = 64 * wh
                for half in range(2):
                    ps = psP[half]
                    pk = KEY("psP", half)
                    for li in range(32):
                        P.matmul(ps[:, 0:127], W1[base:base + 64, li, half * 128:(half + 1) * 128],
                                 kv3[base:base + 64, (li // 16):(li // 16) + 127, li % 16],
                                 start=(li == 0), stop=(li == 31), r=["W1", KEY("fmD", 2)], w=[pk])
                    for li in range(32):
                        P.matmul(psX[:, 0:1], W1[base:base + 64, li, half * 128:(half + 1) * 128], posT[base:base + 64, li:li + 1],
                                 start=(li == 0), stop=(li == 31), r=["W1", "posT"], w=["psX"])
                    P.copy("dve", hb[:, 2 * wh + half:2 * wh + half + 1], psX[:, 0:1], r=["psX"], w=["hb"])
                    P.activation(hs[:, wh, half, 0:127], ps[:, 0:127], AF.Silu, bias=hb[:, 2 * wh + half:2 * wh + half + 1],
                                 r=[pk, "hb"], w=["hs"])
            for half in range(2):
                P.matmul(psM[:, 0:127], w2k[:, half, :], hs[:, 0, half, 0:127], start=(half == 0), stop=(half == 1), r=["w2k", "hs"], w=["psM"])
            P.copy("dve", kcmp[:, 0:127], psM[:, 0:127], r=["psM"], w=["kcmp"])
            for half in range(2):
                P.matmul(psX[0:127, 0:64], hs[:, 1, half, 0:127], w2v[:, half, :], start=(half == 0), stop=(half == 1), r=["w2v", "hs"], w=["psX"])
            P.copy("dve", vcmp[0:127, 0:64], psX[0:127, 0:64], r=["psX"], w=["vcmp"])
            P.barrier()
            P.flush()
        with ExitStack() as p2:
            tabm = sb(p2, "tabmD", [128, 4, S], BF16)
            tabm_ops = [P.dma(tabm[:, h, :], scr["tabM"][h]) for h in range(4)]
            ovx = sb(p2, "ovxD", [128, 33], BF16)
            selF = sb(p2, "selFD", [128, NT, 32])
            selA = sb(p2, "selAD", [128, NT, 32])
            selN = sb(p2, "selND", [128, NT, 32])
            imp = sb(p2, "impD", [128, NT, 32])
            v1 = sb(p2, "v1D", [128, 32])
            v2 = sb(p2, "v2D", [128, 32])
            m8 = sb(p2, "m8D", [128, 16])
            selm = sb(p2, "selmD", [128, 32], BF16)
            P.dma(ovx[0:127, :], dr["ovx"], w=["ovx"], eng="pool")
            P.dma(selF[:], dr["selF"].rearrange("p (t j) -> p t j", j=32), w=["selF"])
            P.dma(selA[:], dr["selA"].rearrange("p (t j) -> p t j", j=32), w=["selA"])
            P.dma(selN[:], dr["selN"].rearrange("p (t j) -> p t j", j=32), w=["selN"])
            psTb = psum(p2, "psTbD", [128, 1024], BF16)
            for b in range(4):
                ob = psO[b % 2][:, 0:260].rearrange("p (q d) -> p q d", q=4)
                okey = KEY("psO", b % 2)
                o2 = psM[:, 0:132].rearrange("p (q d) -> p q d", q=4)
                for h in range(4):
                    pair = h // 2
                    base = 64 * (h % 2)
                    ss = psS[h % 2]
                    skey = KEY("psS", h % 2)
                    i = h % 2
                    P.matmul(ss[0:127, :], kcmp[base:base + 64, 0:127], qT[pair][base:base + 64, 512 * b:512 * (b + 1)], r=["kcmp"], w=[skey])
                    P.activation(Ef[i][0:127, :], ss[0:127, :], AF.Exp, scale=0.125, r=[skey], w=[KEY("Ef", i)])
                    P.tt("dve", Pb[i][0:127, :], Ef[i][0:127, :], tabm[0:127, h, 512 * b:512 * (b + 1)], ALU.mult, r=[KEY("Ef", i)], w=[KEY("Pb", i)], after=tabm_ops)
                    for qi in range(4):
                        P.matmul(ob[:, qi, :], Pb[i][0:127, qi * 128:(qi + 1) * 128], vcmp[0:127, :], start=(qi == 0), stop=True,
                                 r=[KEY("Pb", i), "vcmp"], w=[okey], sgc=True)
                    for qi in range(4):
                        P.matmul(o2[:, qi, :], Pb[i][0:127, qi * 128:(qi + 1) * 128], ovx[0:127, :], start=(qi == 0), stop=True,
                                 r=[KEY("Pb", i), "ovx"], w=["psM"], sgc=True)
                    bsl = slice(4 * b, 4 * b + 4)
                    P.ts("dve", sm[:, 0:4].unsqueeze(2), ob[:, :, 64:65], 1e-30, None, ALU.max, r=[okey], w=["sm0"])
                    P.op("dve", lambda e: e.reciprocal(sm[:, 4:8], sm[:, 0:4]), r=["sm0"], w=["sm1"])
                    P.tt("dve", sm[:, 8:12], sm[:, 4:8], gate[:, bsl, 3 * h], ALU.mult, r=["sm1", "gate"], w=["sm2"])
                    P.tt("dve", acc[:, bsl, h * 64:(h + 1) * 64], ob[:, :, 0:64], sm[:, 8:12].unsqueeze(2).to_broadcast([128, 4, 64]), ALU.mult,
                         r=[okey, "sm2"], w=[KEY("acc", b)])
                    if h == 0:
                        P.tt("dve", imp[:, bsl, :], o2[:, :, 0:32], sm[:, 4:8].unsqueeze(2).to_broadcast([128, 4, 32]), ALU.mult,
                             r=["psM", "sm1"], w=["imp"])
                    else:
                        P.tt("dve", tmpD[:, :, 0:32], o2[:, :, 0:32], sm[:, 4:8].unsqueeze(2).to_broadcast([128, 4, 32]), ALU.mult,
                             r=["psM", "sm1"], w=["tmpD"])
                        P.tt("dve", imp[:, bsl, :], imp[:, bsl, :], tmpD[:, :, 0:32], ALU.add, r=["tmpD", "imp"], w=["imp"])
                for qi in range(4):
                    qt = 4 * b + qi
                    P.tt("dve", v1[:], imp[:, qt, :], selF[:, qt, :], ALU.max, r=["imp", "selF"], w=["v1"])
                    P.tt("dve", v1[:], v1[:], selA[:, qt, :], ALU.mult, r=["v1", "selA"], w=["v1"])
                    P.tt("dve", v1[:], v1[:], selN[:, qt, :], ALU.add, r=["v1", "selN"], w=["v1"])
                    P.op("dve", lambda e: e.max(out=m8[:, 0:8], in_=v1[:]), r=["v1"], w=["m8a"])
                    P.op("dve", lambda e: e.match_replace(out=v2[:], in_to_replace=m8[:, 0:8], in_values=v1[:], imm_value=-3.0e38),
                         r=["v1", "m8a"], w=["v2"])
                    P.op("dve", lambda e: e.max(out=m8[:, 8:16], in_=v2[:]), r=["v2"], w=["m8b"])
                    P.ts("dve", selm[:], v1[:], m8[:, 15:16], None, ALU.is_ge, r=["v1", "m8b"], w=["selm"])
                    P.transpose(psTb[0:32, 0:128], selm[:], ident_b[:], r=["selm", "ident_b"], w=["psTb"])
                    P.copy("act", selT[:, qt * 128:(qt + 1) * 128], psTb[0:32, 0:128], r=["psTb"], w=["selT"])
            P.barrier()
            P.flush()
        with ExitStack() as p3:
            tab = sb(p3, "tabSs", [128, 4, TW_C], BF16)
            ind = sb(p3, "indD", [32, NT, 128], BF16)
            psX3 = psum(p3, "psX3", [128, 512])
            pmring = [psM, psP[0], psP[1]]
            P.dma(ind[:], dr["ind"].rearrange("p (t k) -> p t k", k=128), w=["ind"], eng="pool")
            for h in range(4):
                P.dma(tab[:, h, :], scr["tabS"][h], w=["tab"], eng="pool")
            for h in range(4):
                pair = h // 2
                base = 64 * (h % 2)
                cnt = [0]

                mcnt = [0]

                def prep(b_, kt, qs, w, mcnt=mcnt):
                    j = mcnt[0] % 3
                    mcnt[0] += 1
                    pm = pmring[j]
                    mkey = KEY("psMr", j)
                    P.matmul(pm[:, 0:w], ind[:, kt, :], selT[:, qs:qs + w], r=["ind", "selT"], w=[mkey])
                    return pm, mkey

                def make_P(b_, kt, qs, w, ss, skey, h=h, cnt=cnt, aux=None):
                    i = cnt[0] % 3
                    cnt[0] += 1
                    d0 = qs // 128 - kt
                    pm, mkey = aux
                    if kt >= 4 * b_ - 1:
                        P.activation(Ef[i][:, 0:w], ss[:, 0:w], AF.Exp, scale=0.125, r=[skey], w=[KEY("Ef", i)])
                        P.tt("dve", Ef[i][:, 0:w], Ef[i][:, 0:w], tab[:, h, 128 * d0:128 * d0 + w], ALU.mult,
                             r=[KEY("Ef", i), "tab"], w=[KEY("Ef", i)])
                        P.tt("dve", Pb[i][:, 0:w], Ef[i][:, 0:w], pm[:, 0:w], ALU.mult, r=[KEY("Ef", i), mkey], w=[KEY("Pb", i)])
                    else:
                        P.activation(Eb[i][:, 0:w], ss[:, 0:w], AF.Exp, scale=0.125, bias=rbrow[:, 8 + h:9 + h],
                                     r=[skey, "rbrow"], w=[KEY("Eb", i)])
                        P.tt("dve", Pb[i][:, 0:w], Eb[i][:, 0:w], pm[:, 0:w], ALU.mult, r=[KEY("Eb", i), mkey], w=[KEY("Pb", i)])
                    return Pb[i], KEY("Pb", i)

                def post(b_, ob, okey, h=h):
                    bsl = slice(4 * b_, 4 * b_ + 4)
                    P.op("dve", lambda e: e.reciprocal(sm[:, 4:8].unsqueeze(2), ob[:, :, 64:65]), r=[okey], w=["sm1"])
                    P.tt("dve", sm[:, 8:12], sm[:, 4:8], gate[:, bsl, 3 * h + 1], ALU.mult, r=["sm1", "gate"], w=["sm2"])
                    P.tt("dve", tmpD[:], ob[:, :, 0:64], sm[:, 8:12].unsqueeze(2).to_broadcast([128, 4, 64]), ALU.mult,
                         r=[okey, "sm2"], w=["tmpD"])
                    P.tt("pool", acc[:, bsl, h * 64:(h + 1) * 64], acc[:, bsl, h * 64:(h + 1) * 64], tmpD[:], ALU.add,
                         r=["tmpD", KEY("acc", b_)], w=[KEY("acc", b_)])

                dense_attn(psS + [psX3], psO,
                           lambda kt: ksT[base:base + 64, kt * 128:(kt + 1) * 128],
                           lambda qs, w: qT[pair][base:base + 64, qs:qs + w],
                           lambda kt: vS[:, kt, :],
                           lambda b_: list(range(0, 4 * b_ + 4)), make_P, post, ["vS"], prep=prep)
            P.barrier()
            P.flush()
        with ExitStack() as p4:
            tab = sb(p4, "tabWs", [128, 4, TW_W], BF16)
            for h in range(4):
                P.dma(tab[:, h, :], scr["tabW"][h], w=["tab"], eng="pool")
            for h in range(4):
                pair = h // 2
                base = 64 * (h % 2)
                cnt = [0]

                def make_P(b_, kt, qs, w, ss, skey, h=h, cnt=cnt):
                    i = cnt[0] % 3
                    cnt[0] += 1
                    d0 = qs // 128 - kt
                    P.activation(Ef[i][:, 0:w], ss[:, 0:w], AF.Exp, scale=0.125, r=[skey], w=[KEY("Ef", i)])
                    P.tt("dve", Pb[i][:, 0:w], Ef[i][:, 0:w], tab[:, h, 128 * d0:128 * d0 + w], ALU.mult,
                         r=[KEY("Ef", i), "tab"], w=[KEY("Pb", i)])
                    return Pb[i], KEY("Pb", i)

                def post(b_, ob, okey, h=h):
                    bsl = slice(4 * b_, 4 * b_ + 4)
                    P.op("dve", lambda e: e.reciprocal(sm[:, 4:8].unsqueeze(2), ob[:, :, 64:65]), r=[okey], w=["sm1"])
                    P.tt("dve", sm[:, 8:12], sm[:, 4:8], gate[:, bsl, 3 * h + 2], ALU.mult, r=["sm1", "gate"], w=["sm2"])
                    P.tt("dve", tmpD[:], ob[:, :, 0:64], sm[:, 8:12].unsqueeze(2).to_broadcast([128, 4, 64]), ALU.mult,
                         r=[okey, "sm2"], w=["tmpD"])
                    P.tt("pool", mixed[:, bsl, 768 + h * 64:768 + (h + 1) * 64], acc[:, bsl, h * 64:(h + 1) * 64], tmpD[:], ALU.add,
                         r=["tmpD", KEY("acc", b_)], w=[KEY("mixed", b_)])

                dense_attn(psS + psP, psO,
                           lambda kt: kwT[base:base + 64, kt * 128:(kt + 1) * 128],
                           lambda qs, w: qT[pair][base:base + 64, qs:qs + w],
                           lambda kt: vW[:, kt, :],
                           lambda b_: list(range(max(0, 4 * b_ - 4), 4 * b_ + 4)), make_P, post, ["vW"])
            P.barrier()
            P.flush()


_NC_CACHE = {}


def layout_weights(inputs):
    out = {}
    for k in WEIGHT_SHAPES:
        v = np.asarray(inputs[k], dtype=np.float32)
        if k == "a_conv":
            v = v.transpose(0, 2, 1)
        elif k == "d_cmp_pos":
            v = v.transpose(0, 1, 3, 2)
        elif k in ("b_out", "ln1_g", "ln1_b", "b_ff2", "ln2_g", "ln2_b", "b_ff1"):
            v = v.reshape(2, -1, 128).transpose(0, 2, 1)
        out[k] = np.ascontiguousarray(v)
    return out


def kernel(**inputs):
    n_cores = 8
    x = np.ascontiguousarray(inputs["x"], dtype=np.float32)
    consts = _host_consts()
    if "nc" not in _NC_CACHE:
        _NC_CACHE["nc"] = build_nc()
    nc = _NC_CACHE["nc"]
    in_maps = []
    wl = layout_weights(inputs)
    for c in range(n_cores):
        m = {"x": np.ascontiguousarray(x[2 * c:2 * c + 2])}
        for k in WEIGHT_SHAPES:
            m[k] = wl[k]
        for k, v in consts.items():
            m["k_" + k] = v
        in_maps.append(m)
    res = run_bass_kernel_spmd(nc, in_maps, core_ids=list(range(n_cores)))
    return np.concatenate([np.asarray(r["out"]) for r in res.results], axis=0).astype(np.float32)
```

```python
import math
import numpy as np
from contextlib import ExitStack
import concourse.bass as bass
import concourse.mybir as mybir
from concourse.bass_utils import run_bass_kernel_spmd

F32 = mybir.dt.float32
BF16 = mybir.dt.bfloat16
AF = mybir.ActivationFunctionType
ALU = mybir.AluOpType
AX = mybir.AxisListType

S = 2048
D = 1024
NT = 16
DFF = 4096
ALPHA = (2 * 2) ** 0.25
LN_EPS = 1e-5
STRICT_SAME_ENGINE = True
IN_SPLITS = (('a_q', 128), ('a_k', 128), ('a_v', 256), ('a_i', 4), ('a_f', 4), ('a_o', 256),
             ('b_q', 256), ('b_k', 256), ('b_v', 256),
             ('c_q', 256), ('c_k', 64), ('c_v', 64), ('c_iq', 256), ('c_ik', 64), ('c_iw', 4),
             ('d_q', 256), ('d_kc', 64), ('d_vc', 64), ('d_ks', 64), ('d_vs', 64), ('d_kw', 64),
             ('d_vw', 64), ('d_g', 12))
COL = {}
_o = 0
for _n, _w in IN_SPLITS:
    COL[_n] = _o
    _o += _w
D_IN = _o


class _Op:
    __slots__ = ("eng", "dma", "fn", "deps", "signals", "sem", "val", "nd", "ring", "bg")

    def __init__(self, eng, dma, fn):
        self.eng = eng
        self.dma = dma
        self.fn = fn
        self.deps = []
        self.signals = dma
        self.sem = None
        self.val = 0
        self.nd = 0
        self.ring = ""
        self.bg = False


class Prog:
    ENGS = ("pe", "act", "dve", "pool", "sp")
    KD = 8

    def __init__(self, nc, es):
        self.nc = nc
        self.pending = {e: [] for e in self.ENGS}
        self.lastw = {}
        self.readers = {}
        self.dmas = {}
        self.lastc = {e: None for e in self.ENGS}
        self.cnt = {e: 0 for e in self.ENGS}
        self.seen = {e: {} for e in self.ENGS}
        self.csem = {}
        self.dsem = {}
        self.nops = 0
        for e in self.ENGS:
            self.csem[e] = es.enter_context(nc.semaphore("c_" + e))
            for i in range(self.KD):
                self.dsem[(e, "", i)] = es.enter_context(nc.semaphore("d_%s%d" % (e, i)))
        for i in range(self.KD):
            self.dsem[("pool", "bg", i)] = es.enter_context(nc.semaphore("dbg_%d" % i))

    def op(self, eng, fn, r=(), w=(), dma=False, ring="", after=(), bg=False):
        o = _Op(eng, dma, fn)
        o.ring = ring
        o.bg = bg
        self.nops += 1
        deps = {}

        def add(d, raw):
            if d is None:
                return
            if (not d.dma) and (not dma) and d.eng == eng:
                if eng == "pe" or (not raw and not STRICT_SAME_ENGINE):
                    return
            deps[id(d)] = d

        for k in r:
            add(self.lastw.get(k), True)
        for k in w:
            add(self.lastw.get(k), False)
            for d in self.readers.get(k, ()):
                add(d, False)
        for d in after:
            deps[id(d)] = d
        if dma:
            lst = self.dmas.setdefault((eng, ring), [])
            o.nd = len(lst)
            if o.nd >= self.KD:
                d = lst[o.nd - self.KD]
                deps[id(d)] = d
            lst.append(o)
        else:
            self.lastc[eng] = o
        for d in deps.values():
            d.signals = True
        o.deps = list(deps.values())
        for k in r:
            lst = self.readers.setdefault(k, [])
            lst.append(o)
            if len(lst) > 16:
                keep = {}
                for x in lst:
                    keep[(x.eng, x.dma, x.ring, x.nd % self.KD if x.dma else 0)] = x
                self.readers[k] = list(keep.values())
        for k in w:
            self.lastw[k] = o
            self.readers[k] = []
        self.pending[eng].append(o)
        return o

    def barrier(self, all_rings=False):
        deps = []
        for e in self.ENGS:
            if self.lastc[e] is not None:
                deps.append(self.lastc[e])
        for (e, ring), lst in self.dmas.items():
            if ring == "" or all_rings:
                deps.extend(lst[-self.KD:])
        for d in deps:
            d.signals = True
        for e in self.ENGS:
            o = _Op(e, False, lambda x: x.nop())
            o.deps = list(deps)
            self.pending[e].append(o)
            self.lastc[e] = o
        self.lastw = {}
        self.readers = {}

    def flush(self):
        nc = self.nc
        for e in self.ENGS:
            for o in self.pending[e]:
                if o.dma:
                    o.sem = self.dsem[(e, o.ring, o.nd % self.KD)]
                    o.val = 16 * (o.nd // self.KD + 1)
                elif o.signals:
                    self.cnt[e] += 1
                    o.sem = self.csem[e]
                    o.val = self.cnt[e]
        pend = self.pending
        self.pending = {e: [] for e in self.ENGS}
        with nc.Block() as block:
            def run(e, engobj):
                seen = self.seen[e]
                for o in pend[e]:
                    for d in o.deps:
                        key = id(d.sem)
                        if seen.get(key, 0) >= d.val:
                            continue
                        engobj.wait_ge(d.sem, d.val)
                        seen[key] = d.val
                    ins = o.fn(engobj)
                    if o.signals:
                        ins.then_inc(o.sem, 16 if o.dma else 1)

            @block.tensor
            def _(x):
                run("pe", x)

            @block.scalar
            def _(x):
                run("act", x)

            @block.vector
            def _(x):
                run("dve", x)

            @block.gpsimd
            def _(x):
                run("pool", x)

            @block.sync
            def _(x):
                run("sp", x)

    def dma(self, out, in_, r=(), w=(), eng="sp", ring="", after=(), **kw):
        return self.op(eng, lambda e: e.dma_start(out=out, in_=in_, **kw), r, w, dma=True, ring=ring, after=after)

    def matmul(self, out, lhsT, rhs, start=True, stop=True, r=(), w=(), sgc=False):
        if sgc:
            return self.op("pe", lambda e: e.matmul(out, lhsT, rhs, start=start, stop=stop, skip_group_check=True), r, w)
        return self.op("pe", lambda e: e.matmul(out, lhsT, rhs, start=start, stop=stop), r, w)

    def transpose(self, out, in_, ident, r=(), w=()):
        return self.op("pe", lambda e: e.transpose(out, in_, ident), r, w)

    def activation(self, out, in_, func, bias=None, scale=None, r=(), w=(), accum_out=None):
        kw = {}
        if bias is not None:
            kw["bias"] = bias
        if scale is not None:
            kw["scale"] = scale
        if accum_out is not None:
            kw["accum_out"] = accum_out
        return self.op("act", lambda e: e.activation(out, in_, func, **kw), r, w)

    def tt(self, eng, out, in0, in1, op, r=(), w=(), after=()):
        return self.op(eng, lambda e: e.tensor_tensor(out, in0, in1, op), r, w, after=after)

    def ts(self, eng, out, in0, s1, s2, op0, op1=None, r=(), w=(), accum_out=None):
        kw = {}
        if accum_out is not None:
            kw["accum_out"] = accum_out
        if op1 is None:
            return self.op(eng, lambda e: e.tensor_scalar(out, in0, s1, s2, op0, **kw), r, w)
        return self.op(eng, lambda e: e.tensor_scalar(out, in0, s1, s2, op0, op1, **kw), r, w)

    def stt(self, eng, out, in0, scalar, in1, op0, op1, r=(), w=()):
        return self.op(eng, lambda e: e.scalar_tensor_tensor(out, in0, scalar, in1, op0, op1), r, w)

    def copy(self, eng, out, in_, r=(), w=()):
        if eng == "act":
            return self.op(eng, lambda e: e.copy(out, in_), r, w)
        return self.op(eng, lambda e: e.tensor_copy(out, in_), r, w)

    def memset(self, eng, ap, val, w=()):
        return self.op(eng, lambda e: e.memset(ap, val), (), w)


def _t5_bucket(d):
    n = np.maximum(d, 0)
    nf = np.maximum(n, 16).astype(np.float32)
    large = 16 + (np.log(nf / np.float32(16)) / np.float32(math.log(128 / 16)) * np.float32(16)).astype(np.int32)
    large = np.minimum(large, 31)
    return np.where(n < 16, n, large)


TW_B = 1024
TW_C = 640
TW_W = 1024
TAB = {
    "B": (TW_B + 127, 127),
    "C": (TW_C + 127, 127),
    "W": (TW_W + 127, 127),
    "M": (2048 + 16 * 126, 2047),
}
TABPAD = {k: ((v[0] + 511) // 512) * 512 for k, v in TAB.items()}


def _host_consts():
    c = {}
    c["ident"] = np.eye(128, dtype=np.float32)
    c["anti"] = np.ascontiguousarray(np.eye(128, dtype=np.float32)[::-1])
    c["anti127"] = np.ascontiguousarray(np.eye(127, dtype=np.float32)[::-1])
    k = np.arange(128)[:, None]
    q = np.arange(128)[None, :]
    c["tri"] = (q >= k).astype(np.float32)
    c["negtri"] = np.where(k >= q, 0.0, -1e30).astype(np.float32)
    qq = np.arange(128)[:, None]
    ss = np.arange(128)[None, :]
    c["negtri"] = np.where(ss <= qq, 0.0, -1e30).astype(np.float32)
    j = np.arange(512)[None, :]
    c["mask16"] = (((j - k) % 16) == 0).astype(np.float32)
    for ty, (n, off) in TAB.items():
        npad = TABPAD[ty]
        d = np.arange(npad) - off
        oh = np.zeros((32, npad), np.float32)
        oh[_t5_bucket(d), np.arange(npad)] = 1.0
        if ty == "B":
            cm = np.zeros(npad, np.float32)
            for r in (1, 4, 16):
                cm += ((d % r == 0) & (d >= 0) & (d <= 128 * r)).astype(np.float32)
        elif ty == "W":
            cm = ((d >= 0) & (d < 512)).astype(np.float32)
        else:
            cm = (d >= 0).astype(np.float32)
        cm[n:] = 0.0
        c["oh_" + ty] = oh
        c["cm_" + ty] = np.ascontiguousarray(np.broadcast_to(cm[None, :], (12, npad)))
    cs = np.arange(127)[:, None] * 16
    ssel = np.arange(32)[None, :] * 64
    ov = np.clip(np.minimum(cs + 32, ssel + 64) - np.maximum(cs, ssel), 0, None).astype(np.float32) / 32.0
    ovx = np.zeros((127, 33), np.float32)
    ovx[:, :32] = ov
    ovx[:, 32] = 1.0
    c["ovx"] = ovx
    ind = np.zeros((32, 16, 128), np.float32)
    for kt in range(16):
        for p in range(128):
            ind[2 * kt + p // 64, kt, p] = 1.0
    c["ind"] = ind.reshape(32, 16 * 128)
    t = np.arange(S)
    cur = t // 64
    jj = np.arange(32)[None, :]
    forced = (jj == 0) | (jj == cur[:, None]) | (jj == cur[:, None] - 1)
    adm = (jj * 64) <= t[:, None]
    fs = np.where(forced, 1e9, 0.0).astype(np.float32)
    c["selF"] = np.ascontiguousarray(fs.reshape(16, 128, 32).transpose(1, 0, 2)).reshape(128, 16 * 32)
    c["selA"] = np.ascontiguousarray(adm.astype(np.float32).reshape(16, 128, 32).transpose(1, 0, 2)).reshape(128, 16 * 32)
    c["selN"] = np.ascontiguousarray(np.where(adm, 0.0, -1e30).astype(np.float32).reshape(16, 128, 32).transpose(1, 0, 2)).reshape(128, 16 * 32)
    return c


CONST_SHAPES = None


def _const_shapes():
    global CONST_SHAPES
    if CONST_SHAPES is None:
        CONST_SHAPES = {k: v.shape for k, v in _host_consts().items()}
    return CONST_SHAPES


WEIGHT_SHAPES = {
    'w_in': (2, 1024, 2904), 'b_in': (2, 2904), 'a_conv': (2, 256, 4), 'a_norm': (2, 256),
    'd_cmp_pos': (2, 2, 64, 32), 'd_cmp_w1': (2, 2, 2048, 256), 'd_cmp_w2': (2, 2, 256, 64),
    'w_out': (2, 1024, 1024), 'b_out': (2, 128, 8), 'ln1_g': (2, 128, 8), 'ln1_b': (2, 128, 8),
    'w_ff1': (2, 1024, 4096), 'b_ff1': (2, 128, 32), 'w_ff2': (2, 4096, 1024), 'b_ff2': (2, 128, 8),
    'ln2_g': (2, 128, 8), 'ln2_b': (2, 128, 8), 'rel_bias': (32, 12),
}


def build_nc(n_seq=2, layers=(0, 1), debug=None, mixers="ABCD", ffn=True):
    nc = bass.Bass("TRN2", target_bir_lowering=False)
    dr = {}
    dr["x"] = nc.dram_tensor("x", [n_seq, S, D], F32, kind="ExternalInput").ap()
    for k, shp in WEIGHT_SHAPES.items():
        dr[k] = nc.dram_tensor(k, list(shp), F32, kind="ExternalInput").ap()
    for k, shp in _const_shapes().items():
        dr[k] = nc.dram_tensor("k_" + k, list(shp), F32, kind="ExternalInput").ap()
    out_d = nc.dram_tensor("out", [n_seq, S, D], F32, kind="ExternalOutput").ap()
    tv = {ty: nc.dram_tensor("tv_" + ty, [12, TABPAD[ty]], F32, kind="Internal").ap() for ty in TAB}
    tabB = nc.dram_tensor("tabB", [4, 128, TW_B], F32, kind="Internal").ap()
    tabC = nc.dram_tensor("tabC", [4, 128, TW_C], F32, kind="Internal").ap()
    tabS = nc.dram_tensor("tabS", [4, 128, TW_C], F32, kind="Internal").ap()
    tabW = nc.dram_tensor("tabW", [4, 128, TW_W], F32, kind="Internal").ap()
    tabM = nc.dram_tensor("tabM", [4, 128, 2048], BF16, kind="Internal").ap()
    scrA = nc.dram_tensor("scrA", [3, 4, S], F32, kind="Internal").ap()
    wf1b = nc.dram_tensor("wf1b", [2, 8, 128, 8 * 512], BF16, kind="Internal").ap()
    wf2b = nc.dram_tensor("wf2b", [2, 8, 128, 32 * 128], BF16, kind="Internal").ap()
    bgops = {}

    def dap(t, offset, pattern):
        return bass.AP(t.tensor, offset, pattern)

    KEY = lambda *a: a

    with ExitStack() as es:
        P = Prog(nc, es)

        uid = [0]

        def sb(stack, name, shape, dt=F32):
            uid[0] += 1
            return stack.enter_context(nc.sbuf_tensor("s%d_%s" % (uid[0], name), list(shape), dt))

        def psum(stack, name, shape, dt=F32):
            uid[0] += 1
            return stack.enter_context(nc.psum_tensor("p%d_%s" % (uid[0], name), list(shape), dt))

        xT = sb(es, "xT", [128, 8, S])
        ident_f = sb(es, "ident_f", [128, 128])
        ident_b = sb(es, "ident_b", [128, 128], BF16)
        ones_f = sb(es, "ones_f", [128, 128])
        rbrow = sb(es, "rbrow", [128, 12])
        P.dma(ident_f[:], dr["ident"], w=["ident_f"])
        P.dma(ident_b[:], dr["ident"], w=["ident_b"], eng="pool")
        P.memset("dve", ones_f[:], 1.0, w=["ones_f"])
        P.dma(rbrow[:], dap(dr["rel_bias"], 31 * 12, [[0, 128], [1, 12]]), w=["rbrow"])

        with ExitStack() as ph:
            rb = sb(ph, "rb", [32, 12])
            anti = sb(ph, "anti", [128, 128])
            anti127 = sb(ph, "anti127", [127, 127])
            oh = sb(ph, "oh", [32, 512])
            cm = sb(ph, "cm", [12, 512])
            vv = sb(ph, "vv", [12, 512])
            tps_ = [sb(ph, "tp%d" % i, [128, 2048]) for i in range(2)]
            tfs_ = [sb(ph, "tf%d" % i, [128, 2048]) for i in range(2)]
            psts = [pst_ for pst_ in (psum(ph, "pstb", [128, 512]),)]
            pst = psum(ph, "pst", [128, 512])
            P.dma(rb[:], dr["rel_bias"], w=["rb"])
            P.dma(anti[:], dr["anti"], w=["anti"])
            P.dma(anti127[:], dr["anti127"], w=["anti127"])
            for ty in TAB:
                npad = TABPAD[ty]
                for c0 in range(0, npad, 512):
                    P.dma(oh[:], dr["oh_" + ty][:, c0:c0 + 512], w=["oh"])
                    P.dma(cm[:], dr["cm_" + ty][:, c0:c0 + 512], w=["cm"])
                    P.matmul(pst[0:12, :], rb[:], oh[:], r=["rb", "oh"], w=["pst"])
                    P.activation(vv[:], pst[0:12, :], AF.Exp, r=["pst"], w=["vv"])
                    P.tt("dve", vv[:], vv[:], cm[:], ALU.mult, r=["vv", "cm"], w=["vv"])
                    P.dma(tv[ty][:, c0:c0 + 512], vv[:], r=["vv"], w=["tv" + ty])
            jobs = []
            for h in range(4):
                jobs.append(("B", h, tabB[h], TW_B, 1, 128))
                jobs.append(("C", 4 + h, tabC[h], TW_C, 1, 128))
                jobs.append(("C", 8 + h, tabS[h], TW_C, 1, 128))
                jobs.append(("W", 8 + h, tabW[h], TW_W, 1, 128))
                jobs.append(("M", 8 + h, tabM[h][0:127, :], 2048, 16, 127))
            nj = 0
            for ji, (ty, row, dst, W, pstep, npart) in enumerate(jobs):
                tp = tps_[ji % 2]
                tf = tfs_[ji % 2]
                tpk = KEY("tp", ji % 2)
                tfk = KEY("tf", ji % 2)
                src = dap(tv[ty], row * TABPAD[ty], [[pstep, npart], [1, W]])
                P.dma(tp[0:npart, 0:W], src, r=["tv" + ty], w=[tpk])
                fl = anti if npart == 128 else anti127
                for c0 in range(0, W, 512):
                    wd = min(512, W - c0)
                    pq = (pst, psts[0])[nj % 2]
                    pqk = KEY("pst", nj % 2)
                    nj += 1
                    P.matmul(pq[0:npart, 0:wd], fl[:], tp[0:npart, c0:c0 + wd], r=[tpk, "anti", "anti127"], w=[pqk])
                    P.copy("dve" if nj % 2 else "act", tf[0:npart, c0:c0 + wd], pq[0:npart, 0:wd], r=[pqk], w=[tfk])
                P.dma(dst, tf[0:npart, 0:W], r=[tfk], w=["tabs"], eng=("pool" if ty == "M" else "sp"))
            P.barrier()
            P.flush()

        for si in range(n_seq):
            for l in layers:
                _layer(nc, P, dr, out_d, si, l, layers, xT, ident_f, ident_b, ones_f, rbrow,
                       dict(tabB=tabB, tabC=tabC, tabS=tabS, tabW=tabW, tabM=tabM, scrA=scrA, wf1b=wf1b, wf2b=wf2b, bgops=bgops),
                       debug, mixers, ffn, sb, psum, dap)
        P.barrier(all_rings=True)
        P.flush()
    return nc


def _layer(nc, P, dr, out_d, si, l, layers, xT, ident_f, ident_b, ones_f, rbrow, scr,
           debug, mixers, ffn, sb, psum, dap):
    first_layer = (l == layers[0])
    last_layer = (l == layers[-1])
    KEY = lambda *a: a

    if first_layer:
        with ExitStack() as ph:
            xin = [sb(ph, "xin%d" % i, [128, D]) for i in range(2)]
            pt = [psum(ph, "ptx%d" % i, [128, 512]) for i in range(2)]
            n = 0
            for t in range(NT):
                xi = xin[t % 2]
                P.dma(xi[:], dr["x"][si, t * 128:(t + 1) * 128, :], w=[KEY("xin", t % 2)])
                for g in range(2):
                    p = pt[n % 2]
                    n += 1
                    for j in range(4):
                        kc = g * 4 + j
                        P.transpose(p[:, j * 128:(j + 1) * 128], xi[:, kc * 128:(kc + 1) * 128], ident_f[:],
                                    r=[KEY("xin", t % 2), "ident_f"], w=[KEY("ptx", (n - 1) % 2)])
                    eng = "act" if g == 0 else "dve"
                    P.copy(eng, xT[:, g * 4:(g + 1) * 4, t * 128:(t + 1) * 128],
                           p[:].rearrange("p (j t) -> p j t", j=4),
                           r=[KEY("ptx", (n - 1) % 2)], w=[KEY("xT", t)])
            P.barrier()
            P.flush()

    bgops = scr["bgops"]
    if ("f1", l, 0) not in bgops:
        for hg in range(8):
            bgops[("f1", l, hg)] = P.dma(scr["wf1b"][l, hg].rearrange("p (c n) -> p c n", c=8),
                                         dr["w_ff1"][l, :, hg * 512:(hg + 1) * 512].rearrange("(c p) n -> p c n", p=128),
                                         eng="pool", ring="bg")
        for fc in range(8):
            bgops[("f2", l, fc)] = P.dma(scr["wf2b"][l, fc].rearrange("p (c n) -> p c n", c=32),
                                         dr["w_ff2"][l, :, fc * 128:(fc + 1) * 128].rearrange("(c p) n -> p c n", p=128),
                                         eng="pool", ring="bg")

    with ExitStack() as mx:
        xbf = sb(mx, "xbf", [128, 8, S], BF16)
        mx2 = ExitStack()
        mixed = sb(mx2, "mixed", [128, NT, D], BF16)
        for kc in range(8):
            eng = ("act", "dve", "pool")[kc % 3]
            P.copy(eng, xbf[:, kc, :], xT[:, kc, :], w=["xbf"])
        if debug == "mixed":
            P.memset("pool", mixed[:], 0.0, w=["mixed0"])
        P.barrier()
        P.flush()

        bcol = lambda name, off, n: dap(dr["b_in"], l * D_IN + COL[name] + off, [[1, n], [1, 1]])
        brow = lambda name, off, n: dap(dr["b_in"], l * D_IN + COL[name] + off, [[0, 128], [1, n]])

        def load_w(dst, dcol, name, off, ncols, key):
            src = dr["w_in"][l, :, COL[name] + off:COL[name] + off + ncols].rearrange("(c p) n -> p c n", p=128)
            P.dma(dst[:, :, dcol:dcol + ncols], src, w=[key], eng="pool")

        def fm_proj(wt, wkey, M, psl, pskey, evac):
            for tb in range(4):
                ps = psl[tb % 2]
                for kc in range(8):
                    P.matmul(ps[0:M, :], wt[:, kc, 0:M], xbf[:, kc, tb * 512:(tb + 1) * 512],
                             start=(kc == 0), stop=(kc == 7), r=[wkey, "xbf"], w=[KEY(pskey, tb % 2)])
                evac(tb, ps, KEY(pskey, tb % 2))

        def tm_proj(wt, wkey, N, psl, pskey, evac):
            for t in range(NT):
                ps = psl[t % 2]
                for kc in range(8):
                    P.matmul(ps[:, 0:N], xbf[:, kc, t * 128:(t + 1) * 128], wt[:, kc, 0:N],
                             start=(kc == 0), stop=(kc == 7), r=[wkey, "xbf"], w=[KEY(pskey, t % 2)])
                evac(t, ps, KEY(pskey, t % 2))

        nS = [0]

        def dense_attn(psS, psO, lhs_k, rhs_q, v_rhs, kts_for_block, make_P, post, rkeys, blocks=range(4), prep=None):
            for b in blocks:
                kts = kts_for_block(b)
                ob = psO[b % 2][:, 0:260].rearrange("p (q d) -> p q d", q=4)
                okey = KEY("psO", b % 2)
                steps = []
                for kt in kts:
                    qs = max(512 * b, 128 * kt)
                    steps.append((kt, qs, 512 * (b + 1) - qs))

                NS = len(psS)

                def issue_S(i):
                    kt, qs, w = steps[i]
                    j = nS[0] % NS
                    nS[0] += 1
                    ss = psS[j]
                    skey = KEY("psS", id(ss))
                    P.matmul(ss[:, 0:w], lhs_k(kt), rhs_q(qs, w), r=rkeys, w=[skey])
                    aux = prep(b, kt, qs, w) if prep is not None else None
                    return ss, skey, aux
                q_ = []
                nxt = 0
                while nxt < len(steps) and len(q_) < NS - 1:
                    q_.append(issue_S(nxt))
                    nxt += 1
                firstmm = True
                for i, (kt, qs, w) in enumerate(steps):
                    if nxt < len(steps):
                        q_.append(issue_S(nxt))
                        nxt += 1
                    ss, skey, aux = q_.pop(0)
                    Pt, pkey = make_P(b, kt, qs, w, ss, skey) if aux is None else make_P(b, kt, qs, w, ss, skey, aux=aux)
                    for qi in range((qs - 512 * b) // 128, 4):
                        qt = 4 * b + qi
                        col = qt * 128 - qs
                        P.matmul(ob[:, qi, :], Pt[:, col:col + 128], v_rhs(kt),
                                 start=firstmm, stop=(kt == qt), r=[pkey] + rkeys, w=[okey], sgc=True)
                        firstmm = False
                post(b, ob, okey)

        if "A" in mixers:
            with ExitStack() as ph:
                psP = [psum(ph, "psPA%d" % i, [128, 512]) for i in range(2)]
                psS = [psum(ph, "psSA%d" % i, [128, 512]) for i in range(4)]
                psO = [psum(ph, "psOA%d" % i, [128, 512]) for i in range(2)]
                qk = [sb(ph, "qkA%d" % i, [64, S], BF16) for i in range(4)]
                vaug = sb(ph, "vaugA", [128, NT, 4, 65], BF16)
                osig = sb(ph, "osigA", [128, NT, 256], BF16)
                gn = sb(ph, "gnA", [128, 256])
                tri = sb(ph, "triA", [128, 128], BF16)
                ab_tok = sb(ph, "ab_tok", [128, 4, NT])
                em_tok = sb(ph, "em_tok", [128, 4, NT])
                ph1 = ExitStack()
                ph1.__enter__()
                wt = [sb(ph1, "wA0", [128, 8, 256], BF16)] * 2
                pre = sb(ph1, "preA", [64, S + 3])
                cacc = sb(ph1, "caccA", [64, S])
                cw = sb(ph1, "cwA", [64, 4, 4])
                bq = sb(ph1, "bqA", [64, 4])
                bv = sb(ph1, "bvA", [128, 256])
                bo = sb(ph1, "boA", [128, 256])
                bg = sb(ph1, "bgA", [4, 2])
                gi = sb(ph1, "giA", [4, S])
                gf = sb(ph1, "gfA", [4, S])
                g2 = cacc[0:4, :]
                ones4 = ones_f[0:4, 0:1].to_broadcast([4, S])
                P.dma(tri[:], dr["tri"], w=["tri"], eng="pool")
                P.dma(gn[:], dap(dr["a_norm"], l * 256, [[0, 128], [1, 256]]), w=["gn"])
                P.memset("pool", vaug[:], 1.0, w=["vaug"])
                P.memset("pool", pre[:, 0:3], 0.0, w=["pre"])
                for i in range(4):
                    name = "a_q" if i < 2 else "a_k"
                    off = (i % 2) * 64
                    w_ = wt[i % 2]
                    load_w(w_, 0, name, off, 64, KEY("wA", 0))
                    P.dma(bq[:, i:i + 1], bcol(name, off, 64), w=["bq"])
                    cc = (0 if i < 2 else 128) + off
                    P.dma(cw[:, i, :], dr["a_conv"][l, cc:cc + 64, :], w=["cw"])

                    def evac(tb, ps, pk, i=i):
                        P.activation(pre[:, 3 + tb * 512:3 + (tb + 1) * 512], ps[0:64, :], AF.Identity,
                                     bias=bq[:, i:i + 1], r=[pk, "bq"], w=["pre"])
                    fm_proj(w_, KEY("wA", 0), 64, psP, "psP", evac)
                    P.ts("dve", cacc[:], pre[:, 3:3 + S], cw[:, i, 3:4], None, ALU.mult, r=["pre", "cw"], w=["cacc"])
                    for j in range(3):
                        P.stt("dve", cacc[:], pre[:, j:j + S], cw[:, i, j:j + 1], cacc[:], ALU.mult, ALU.add,
                              r=["pre", "cw", "cacc"], w=["cacc"])
                    P.activation(qk[i][:], cacc[:], AF.Silu, r=["cacc"], w=[KEY("qkA", i)])
                load_w(wt[0], 0, "a_v", 0, 256, KEY("wA", 0))
                P.dma(bv[:], brow("a_v", 0, 256), w=["bv"])

                def evac_v(t, ps, pk):
                    P.tt("dve", vaug[:, t, :, 0:64], ps[:, 0:256].rearrange("p (h d) -> p h d", h=4),
                         bv[:].rearrange("p (h d) -> p h d", h=4), ALU.add, r=[pk, "bv"], w=["vaug"])
                tm_proj(wt[0], KEY("wA", 0), 256, psP, "psP", evac_v)
                load_w(wt[1], 0, "a_o", 0, 256, KEY("wA", 0))
                P.dma(bo[:], brow("a_o", 0, 256), w=["bo"])

                def evac_o(t, ps, pk):
                    P.tt("dve", ps[:, 0:256], ps[:, 0:256], bo[:], ALU.add, r=[pk, "bo"], w=[pk])
                    P.activation(osig[:, t, :], ps[:, 0:256], AF.Sigmoid, r=[pk], w=["osig"])
                tm_proj(wt[1], KEY("wA", 0), 256, psP, "psP", evac_o)
                load_w(wt[0], 0, "a_i", 0, 4, KEY("wA", 0))
                load_w(wt[0], 4, "a_f", 0, 4, KEY("wA", 0))
                P.dma(bg[:, 0:1], bcol("a_i", 0, 4), w=["bg"])
                P.dma(bg[:, 1:2], bcol("a_f", 0, 4), w=["bg"])
                for gidx, gt in ((0, gi), (1, gf)):
                    for tb in range(4):
                        ps = psP[tb % 2]
                        for kc in range(8):
                            P.matmul(ps[0:4, :], wt[0][:, kc, 4 * gidx:4 * gidx + 4], xbf[:, kc, tb * 512:(tb + 1) * 512],
                                     start=(kc == 0), stop=(kc == 7), r=[KEY("wA", 0), "xbf"], w=[KEY("psP", tb % 2)])
                        P.activation(gt[:, tb * 512:(tb + 1) * 512], ps[0:4, :], AF.Identity, bias=bg[:, gidx:gidx + 1],
                                     r=[KEY("psP", tb % 2), "bg"], w=["g%d" % gidx])
                P.activation(gf[:], gf[:], AF.Exp, scale=-1.0, r=["g1"], w=["g1"])
                P.activation(gf[:], gf[:], AF.Ln, bias=1.0, r=["g1"], w=["g1"])
                P.op("dve", lambda e: e.tensor_tensor_scan(g2, ones4, gf[:], 0.0, ALU.mult, ALU.add),
                     r=["g1", "ones_f"], w=["cacc"])
                P.tt("dve", gi[:], gi[:], g2, ALU.add, r=["g0", "cacc"], w=["g0"])
                P.op("dve", lambda e: e.tensor_tensor_scan(gf[:], ones4, gi[:], 0.0, ALU.mult, ALU.max),
                     r=["g0", "ones_f", "g1"], w=["g1"])
                P.tt("dve", g2, gf[:], g2, ALU.subtract, r=["g1", "cacc"], w=["cacc"])
                P.activation(g2, g2, AF.Exp, scale=-1.0, r=["cacc"], w=["cacc"])
                P.ts("dve", gf[:], gf[:], -1.0, None, ALU.mult, r=["g1"], w=["g1"])
                P.ts("dve", gi[:], gi[:], math.log(32 ** -0.5), None, ALU.add, r=["g0"], w=["g0"])
                sA = scr["scrA"]
                P.dma(sA[0], gf[:], r=["g1"], w=["scrA0"])
                for src, dst, rk, wkk in ((gi, ab_tok, "g0", "ab_tok"), (g2, em_tok, "cacc", "em_tok")):
                    pv = psP[0][:, 0:64].rearrange("p (t h) -> p t h", h=4)
                    for t in range(NT):
                        P.transpose(pv[:, t, :], src[0:4, t * 128:(t + 1) * 128], ident_f[0:4, 0:4], r=[rk, "ident_f"], w=[KEY("psP", 0)])
                    P.copy("dve", dst[:].rearrange("p h t -> p t h"), pv, r=[KEY("psP", 0)], w=[wkk])
                P.barrier()
                P.flush()
                ph1.__exit__(None, None, None)
                numA = sb(ph, "numA", [128, NT, 4, 65])
                ph2 = ExitStack()
                negM = sb(ph2, "negMbc", [128, S])
                Wt = [sb(ph2, "WtA%d" % i, [128, 512]) for i in range(3)]
                Pb = [sb(ph2, "PbA%d" % i, [128, 512], BF16) for i in range(3)]
                for h in range(4):
                    pair = h // 2
                    base = 32 * (h % 2)
                    P.dma(negM[:], dap(sA, h * S, [[0, 128], [1, S]]), w=["negM"])
                    cnt = [0]

                    def make_P(b, kt, qs, w, ss, skey, h=h, cnt=cnt):
                        i = cnt[0] % 3
                        cnt[0] += 1
                        P.activation(Wt[i][:, 0:w], negM[:, qs:qs + w], AF.Exp, bias=ab_tok[:, h, kt:kt + 1],
                                     r=["negM", "ab_tok"], w=[KEY("Wt", i)])
                        P.tt("dve", Pb[i][:, 0:w], ss[:, 0:w], Wt[i][:, 0:w], ALU.mult, r=[skey, KEY("Wt", i)], w=[KEY("Pb", i)])
                        if kt >= 4 * b:
                            P.tt("pool", Pb[i][:, 0:128], Pb[i][:, 0:128], tri[:], ALU.mult, r=[KEY("Pb", i), "tri"], w=[KEY("Pb", i)])
                        return Pb[i], KEY("Pb", i)

                    def post(b, ob, okey, h=h):
                        P.copy("act", numA[:, 4 * b:4 * b + 4, h, :], ob[:, :, :], r=[okey], w=["numA"])

                    dense_attn(psS, psO,
                               lambda kt: qk[2 + pair][base:base + 32, kt * 128:(kt + 1) * 128],
                               lambda qs, w: qk[pair][base:base + 32, qs:qs + w],
                               lambda kt: vaug[:, kt, h, :],
                               lambda b: list(range(0, 4 * b + 4)), make_P, post,
                               [KEY("qkA", pair), KEY("qkA", 2 + pair), "vaug"])
                P.barrier()
                P.flush()
                ph2.close()
                with ExitStack() as pa:
                    G = NT * 4
                    hh = sb(pa, "hhA", [128, G, 64])
                    tq = sb(pa, "tqA", [128, G // 2, 64])
                    sA_ = sb(pa, "sA_", [128, 6, G])
                    num3 = numA[:].rearrange("p t h d -> p (t h) d")
                    den = num3[:, :, 64]
                    emv = em_tok[:].rearrange("p h t -> p t h")
                    a1 = sA_[:, 0, :]
                    a1v = a1.rearrange("p (t h) -> p t h", h=4)
                    P.ts("dve", a1, den, -1.0, None, ALU.mult, w=["a1"])
                    P.tt("dve", a1, a1, den, ALU.max, r=["a1"], w=["a1"])
                    P.tt("dve", a1v, a1v, emv, ALU.max, r=["a1"], w=["a1"])
                    P.op("dve", lambda e: e.reciprocal(sA_[:, 1, :], a1), r=["a1"], w=["rd"])
                    bc = lambda ap: ap.unsqueeze(2).to_broadcast([128, G, 64])
                    P.tt("dve", hh[:], num3[:, :, 0:64], bc(sA_[:, 1, :]), ALU.mult, r=["rd"], w=["hh"])
                    P.tt("dve", hh[:], hh[:], osig[:].rearrange("p t (h d) -> p (t h) d", h=4), ALU.mult, r=["hh"], w=["hh"])
                    P.op("dve", lambda e: e.tensor_reduce(sA_[:, 2, :], hh[:], AX.X, ALU.add), r=["hh"], w=["s1"])
                    P.ts("dve", sA_[:, 2, :], sA_[:, 2, :], 1.0 / 64, None, ALU.mult, r=["s1"], w=["s1"])
                    P.tt("dve", hh[:], hh[:], bc(sA_[:, 2, :]), ALU.subtract, r=["hh", "s1"], w=["hh"])
                    for hf in range(2):
                        gs = slice(hf * (G // 2), (hf + 1) * (G // 2))
                        P.tt("pool" if hf else "dve", tq[:], hh[:, gs, :], hh[:, gs, :], ALU.mult, r=["hh"], w=["tq"])
                        P.op("dve", lambda e, gs=gs: e.tensor_reduce(sA_[:, 3, gs], tq[:], AX.X, ALU.add), r=["tq"], w=["s2"])
                    P.activation(sA_[:, 4, :], sA_[:, 3, :], AF.Ln, bias=LN_EPS, scale=1.0 / 64, r=["s2"], w=["s4"])
                    P.activation(sA_[:, 4, :], sA_[:, 4, :], AF.Exp, scale=-0.5, r=["s4"], w=["s4"])
                    P.tt("dve", hh[:], hh[:], bc(sA_[:, 4, :]), ALU.mult, r=["hh", "s4"], w=["hh"])
                    P.tt("dve", mixed[:, :, 0:256], hh[:].rearrange("p (t h) d -> p t (h d)", h=4),
                         gn[:].unsqueeze(1).to_broadcast([128, NT, 256]), ALU.mult, r=["hh"], w=["mixedA"])
                    P.barrier()
                    P.flush()

        if "B" in mixers:
            with ExitStack() as ph:
                wt = [sb(ph, "wB%d" % i, [128, 8, 256], BF16) for i in range(2)]
                psP = [psum(ph, "psPB%d" % i, [128, 512]) for i in range(2)]
                psS = [psum(ph, "psSB%d" % i, [128, 512]) for i in range(4)]
                psO = [psum(ph, "psOB%d" % i, [128, 512]) for i in range(2)]
                qT = [sb(ph, "qB%d" % i, [128, S], BF16) for i in range(2)]
                kT = [sb(ph, "kB%d" % i, [128, S], BF16) for i in range(2)]
                vaug = sb(ph, "vaugB", [128, NT, 4, 65], BF16)
                bq = sb(ph, "bqB", [128, 4])
                bv = sb(ph, "bvB", [128, 256])
                tab = sb(ph, "tabB", [128, 4, TW_B], BF16)
                m16 = sb(ph, "m16", [128, 512], BF16)
                Ef = [sb(ph, "EfB%d" % i, [128, 512], BF16) for i in range(3)]
                Eb = [sb(ph, "EbB%d" % i, [128, 512], BF16) for i in range(3)]
                Pb = [sb(ph, "PbB%d" % i, [128, 512], BF16) for i in range(3)]
                sm = sb(ph, "smB", [128, 4])
                P.dma(m16[:], dr["mask16"], w=["m16"], eng="pool")
                for h in range(4):
                    P.dma(tab[:, h, :], scr["tabB"][h], w=["tab"], eng="pool")
                P.memset("pool", vaug[:], 1.0, w=["vaug"])
                for i in range(4):
                    name = "b_q" if i < 2 else "b_k"
                    off = (i % 2) * 128
                    load_w(wt[i % 2], 0, name, off, 128, KEY("wB", i % 2))
                    P.dma(bq[:, i:i + 1], bcol(name, off, 128), w=["bq"])
                    dst = qT[i] if i < 2 else kT[i - 2]

                    def evac(tb, ps, pk, i=i, dst=dst):
                        P.activation(dst[:, tb * 512:(tb + 1) * 512], ps[:, :], AF.Identity, bias=bq[:, i:i + 1],
                                     r=[pk, "bq"], w=[KEY("qkB", i)])
                    fm_proj(wt[i % 2], KEY("wB", i % 2), 128, psP, "psP", evac)
                load_w(wt[0], 0, "b_v", 0, 256, KEY("wB", 0))
                P.dma(bv[:], brow("b_v", 0, 256), w=["bv"])

                def evac_v(t, ps, pk):
                    P.tt("dve", vaug[:, t, :, 0:64], ps[:, 0:256].rearrange("p (h d) -> p h d", h=4),
                         bv[:].rearrange("p (h d) -> p h d", h=4), ALU.add, r=[pk, "bv"], w=["vaug"])
                tm_proj(wt[0], KEY("wB", 0), 256, psP, "psP", evac_v)
                for h in range(4):
                    pair = h // 2
                    base = 64 * (h % 2)
                    cnt = [0]

                    def make_P(b, kt, qs, w, ss, skey, h=h, cnt=cnt):
                        i = cnt[0] % 3
                        cnt[0] += 1
                        d0 = qs // 128 - kt
                        if kt >= 4 * b - 4:
                            P.activation(Ef[i][:, 0:w], ss[:, 0:w], AF.Exp, scale=0.125, r=[skey], w=[KEY("Ef", i)])
                            P.tt("dve", Pb[i][:, 0:w], Ef[i][:, 0:w], tab[:, h, 128 * d0:128 * d0 + w], ALU.mult,
                                 r=[KEY("Ef", i), "tab"], w=[KEY("Pb", i)])
                        else:
                            P.activation(Eb[i][:, 0:w], ss[:, 0:w], AF.Exp, scale=0.125, bias=rbrow[:, h:h + 1],
                                         r=[skey, "rbrow"], w=[KEY("Eb", i)])
                            P.tt("dve", Pb[i][:, 0:w], Eb[i][:, 0:w], m16[:, 0:w], ALU.mult,
                                 r=[KEY("Eb", i), "m16"], w=[KEY("Pb", i)])
                        return Pb[i], KEY("Pb", i)

                    def post(b, ob, okey, h=h):
                        P.op("dve", lambda e: e.reciprocal(sm[:, 0:4].unsqueeze(2), ob[:, :, 64:65]), r=[okey], w=["sm0"])
                        P.tt("dve", mixed[:, 4 * b:4 * b + 4, 256 + h * 64:256 + (h + 1) * 64], ob[:, :, 0:64],
                             sm[:, 0:4].unsqueeze(2).to_broadcast([128, 4, 64]), ALU.mult, r=[okey, "sm0"], w=[KEY("mixed", b)])

                    dense_attn(psS, psO,
                               lambda kt: kT[pair][base:base + 64, kt * 128:(kt + 1) * 128],
                               lambda qs, w: qT[pair][base:base + 64, qs:qs + w],
                               lambda kt: vaug[:, kt, h, :],
                               lambda b: list(range(0, 4 * b + 4)), make_P, post,
                               [KEY("qkB", pair), KEY("qkB", 2 + pair), "vaug"])
                P.barrier()
                P.flush()

        if "C" in mixers:
            _mixer_c(nc, P, dr, l, scr, sb, psum, dap, xbf, mixed, rbrow, ident_b, load_w, bcol, brow,
                     fm_proj, tm_proj, dense_attn, KEY)
        if "D" in mixers:
            _mixer_d(nc, P, dr, l, scr, sb, psum, dap, xbf, mixed, rbrow, ident_b, load_w, bcol, brow,
                     fm_proj, tm_proj, dense_attn, KEY)

        if debug == "mixed":
            for t in range(NT):
                P.dma(out_d[si, t * 128:(t + 1) * 128, :], mixed[:, t, :], r=[KEY("mixed", t), "mixed0"], w=["out"], eng="pool")
            P.barrier()
            P.flush()
            mx2.close()
            return

        with ExitStack() as ph:
            ptb = [psum(ph, "ptb%d" % i, [128, 8, 128], BF16)[:, 0:4, :] for i in range(2)]
            n = 0
            for t in range(NT):
                for g in range(2):
                    p = ptb[n % 2]
                    pk = KEY("ptb", n % 2)
                    n += 1
                    for j in range(4):
                        kc = g * 4 + j
                        P.transpose(p[:, j, :], mixed[:, t, kc * 128:(kc + 1) * 128], ident_b[:], r=["ident_b"], w=[pk])
                    P.copy("act" if g == 0 else "dve", xbf[:, g * 4:(g + 1) * 4, t * 128:(t + 1) * 128], p[:], r=[pk], w=["xbfm"])
            P.barrier()
            P.flush()
        mx2.close()
        _dense_tail(nc, P, dr, out_d, si, l, last_layer, xT, xbf, ident_f, ones_f, sb, psum, dap, KEY, ffn, debug, scr=scr)


def _ln_fm(nc, P, xT, ones_f, gcol, bcol_, tmp, st, psl, KEY, xbf_out=None):
    for tb in range(4):
        sl = slice(tb * 512, (tb + 1) * 512)
        p1, p2 = psl
        for kc in range(8):
            P.matmul(p1[:, :], ones_f[:], xT[:, kc, sl], start=(kc == 0), stop=(kc == 7), r=["xTw", "ones_f"], w=["lnp1"])
        for kc in range(8):
            P.activation(tmp[kc % 2][:], xT[:, kc, sl], AF.Square, r=["xTw"], w=[KEY("lnsq", kc % 2)])
            P.matmul(p2[:, :], ones_f[:], tmp[kc % 2][:], start=(kc == 0), stop=(kc == 7), r=[KEY("lnsq", kc % 2), "ones_f"], w=["lnp2"])
        mean, rstd = st
        P.activation(mean[:], p1[:, :], AF.Identity, scale=1.0 / D, r=["lnp1"], w=["mean"])
        P.activation(rstd[:], p2[:, :], AF.Identity, scale=1.0 / D, r=["lnp2"], w=["rstd"])
        P.tt("dve", tmp[0][:], mean[:], mean[:], ALU.mult, r=["mean", KEY("lnsq", 0)], w=[KEY("lnsq", 0)])
        P.tt("dve", rstd[:], rstd[:], tmp[0][:], ALU.subtract, r=["rstd", KEY("lnsq", 0)], w=["rstd"])
        P.activation(rstd[:], rstd[:], AF.Sqrt, bias=LN_EPS, r=["rstd"], w=["rstd"])
        P.op("dve", lambda e: e.reciprocal(rstd[:], rstd[:]), r=["rstd"], w=["rstd"])
        for kc in range(8):
            P.tt("dve", xT[:, kc, sl], xT[:, kc, sl], mean[:], ALU.subtract, r=["xTw", "mean"], w=["xTw"])
            P.tt("pool", xT[:, kc, sl], xT[:, kc, sl], rstd[:], ALU.mult, r=["xTw", "rstd"], w=["xTw"])
            P.ts("dve", xT[:, kc, sl], xT[:, kc, sl], gcol[:, kc:kc + 1], bcol_[:, kc:kc + 1], ALU.mult, ALU.add,
                 r=["xTw", "lng"], w=["xTw"])
            if xbf_out is not None:
                P.copy("act", xbf_out[:, kc, sl], xT[:, kc, sl], r=["xTw"], w=["xbfw"])


def _dense_tail(nc, P, dr, out_d, si, l, last_layer, xT, xbf, ident_f, ones_f, sb, psum, dap, KEY, ffn, debug, scr=None):
    with ExitStack() as ph:
        wo = [sb(ph, "wo%d" % i, [128, 8, 256], BF16) for i in range(2)]
        ps = [psum(ph, "pso%d" % i, [128, 512]) for i in range(4)]
        tmpf = [sb(ph, "tmpo%d" % i, [128, 512]) for i in range(2)]
        bo = sb(ph, "bout", [128, 8])
        g1 = sb(ph, "ln1g", [128, 8])
        b1 = sb(ph, "ln1b", [128, 8])
        mean = sb(ph, "mean", [128, 512])
        rstd = sb(ph, "rstd", [128, 512])
        P.dma(bo[:], dr["b_out"][l], w=["bo"])
        P.dma(g1[:], dr["ln1_g"][l], w=["lng"])
        P.dma(b1[:], dr["ln1_b"][l], w=["lng"])
        n = 0
        for fg in range(4):
            w_ = wo[fg % 2]
            wk = KEY("wo", fg % 2)
            P.dma(w_[:], dr["w_out"][l, :, fg * 256:(fg + 1) * 256].rearrange("(c p) n -> p c n", p=128), w=[wk], eng="pool")
            for f2 in range(2):
                fc = fg * 2 + f2
                for tb in range(4):
                    p = ps[n % 4]
                    pk = KEY("pso", n % 4)
                    tf_ = tmpf[n % 2]
                    tk = KEY("tmpo", n % 2)
                    n += 1
                    sl = slice(tb * 512, (tb + 1) * 512)
                    for kc in range(8):
                        P.matmul(p[:, :], w_[:, kc, f2 * 128:(f2 + 1) * 128], xbf[:, kc, sl], start=(kc == 0), stop=(kc == 7),
                                 r=[wk, "xbfm"], w=[pk])
                    P.activation(tf_[:], p[:, :], AF.Identity, bias=bo[:, fc:fc + 1], r=[pk, "bo"], w=[tk])
                    P.stt("dve", xT[:, fc, sl], xT[:, fc, sl], ALPHA, tf_[:], ALU.mult, ALU.add, r=[tk, "xTw"], w=["xTw"])
        _ln_fm(nc, P, xT, ones_f, g1, b1, tmpf, (mean, rstd), (ps[0], ps[1]), KEY, xbf_out=xbf)
        P.barrier()
        P.flush()
    if debug == "x1":
        _store_out(nc, P, out_d, si, xT, ident_f, sb, psum, KEY)
        return
    if ffn:
        with ExitStack() as ph:
            hT = sb(ph, "hT", [128, 32, 1024], BF16)
            w1 = [sb(ph, "w1_%d" % i, [128, 8, 512], BF16) for i in range(2)]
            w2 = [sb(ph, "w2_%d" % i, [128, 32, 128], BF16) for i in range(2)]
            ps = [psum(ph, "psf%d" % i, [128, 512]) for i in range(4)]
            tmpf = [sb(ph, "tmpf%d" % i, [128, 512]) for i in range(2)]
            b1 = sb(ph, "bff1", [128, 32])
            b2 = sb(ph, "bff2", [128, 8])
            g2 = sb(ph, "ln2g", [128, 8])
            bb2 = sb(ph, "ln2b", [128, 8])
            mean = sb(ph, "mean2", [128, 512])
            rstd = sb(ph, "rstd2", [128, 512])
            P.dma(b1[:], dr["b_ff1"][l], w=["b1"])
            P.dma(b2[:], dr["b_ff2"][l], w=["b2"])
            P.dma(g2[:], dr["ln2_g"][l], w=["lng"])
            P.dma(bb2[:], dr["ln2_b"][l], w=["lng"])
            n = 0
            for half in range(2):
                t0 = half * 1024
                for hg in range(8):
                    w_ = w1[hg % 2]
                    wk = KEY("w1", hg % 2)
                    P.dma(w_[:], scr["wf1b"][l, hg].rearrange("p (c n) -> p c n", c=8), w=[wk], after=[scr["bgops"][("f1", l, hg)]])
                    for h4 in range(4):
                        hc = hg * 4 + h4
                        for tb in range(2):
                            p = ps[n % 4]
                            pk = KEY("psf", n % 4)
                            tf_ = tmpf[n % 2]
                            tk = KEY("tmpf", n % 2)
                            n += 1
                            sl = slice(t0 + tb * 512, t0 + (tb + 1) * 512)
                            for kc in range(8):
                                P.matmul(p[:, :], w_[:, kc, h4 * 128:(h4 + 1) * 128], xbf[:, kc, sl], start=(kc == 0), stop=(kc == 7),
                                         r=[wk, "xbfw"], w=[pk])
                            P.activation(tf_[:], p[:, :], AF.Relu, bias=b1[:, hc:hc + 1], r=[pk, "b1"], w=[tk])
                            P.tt("pool", hT[:, hc, tb * 512:(tb + 1) * 512], tf_[:], tf_[:], ALU.mult, r=[tk], w=[KEY("hT", hc)])
                for fc in range(8):
                    w_ = w2[fc % 2]
                    wk = KEY("w2", fc % 2)
                    P.dma(w_[:], scr["wf2b"][l, fc].rearrange("p (c n) -> p c n", c=32), w=[wk], after=[scr["bgops"][("f2", l, fc)]])
                    for tb in range(2):
                        p = ps[n % 4]
                        pk = KEY("psf", n % 4)
                        tf_ = tmpf[n % 2]
                        tk = KEY("tmpf", n % 2)
                        n += 1
                        sl = slice(t0 + tb * 512, t0 + (tb + 1) * 512)
                        for kc in range(32):
                            P.matmul(p[:, :], w_[:, kc, :], hT[:, kc, tb * 512:(tb + 1) * 512], start=(kc == 0), stop=(kc == 31),
                                     r=[wk, KEY("hT", kc)], w=[pk])
                        P.activation(tf_[:], p[:, :], AF.Identity, bias=b2[:, fc:fc + 1], r=[pk, "b2"], w=[tk])
                        P.stt("dve", xT[:, fc, sl], xT[:, fc, sl], ALPHA, tf_[:], ALU.mult, ALU.add, r=[tk, "xTw"], w=["xTw"])
            _ln_fm(nc, P, xT, ones_f, g2, bb2, tmpf, (mean, rstd), (ps[0], ps[1]), KEY)
            P.barrier()
            P.flush()
    if last_layer:
        _store_out(nc, P, out_d, si, xT, ident_f, sb, psum, KEY)


def _store_out(nc, P, out_d, si, xT, ident_f, sb, psum, KEY):
    with ExitStack() as ph:
        xo = [sb(ph, "xo%d" % i, [128, 1024]) for i in range(2)]
        pt = [psum(ph, "pto%d" % i, [128, 512]) for i in range(2)]
        n = 0
        for t in range(NT):
            x_ = xo[t % 2]
            xk = KEY("xo", t % 2)
            for g in range(2):
                p = pt[n % 2]
                pk = KEY("pto", n % 2)
                n += 1
                for j in range(4):
                    kc = g * 4 + j
                    P.transpose(p[:, j * 128:(j + 1) * 128], xT[:, kc, t * 128:(t + 1) * 128], ident_f[:], r=["ident_f"], w=[pk])
                P.copy("act" if g == 0 else "dve", x_[:, g * 512:(g + 1) * 512], p[:, :], r=[pk], w=[xk])
            P.dma(out_d[si, t * 128:(t + 1) * 128, :], x_[:], r=[xk], w=["out"])
        P.barrier()
        P.flush()


def _mixer_c(nc, P, dr, l, scr, sb, psum, dap, xbf, mixed, rbrow, ident_b, load_w, bcol, brow,
             fm_proj, tm_proj, dense_attn, KEY):
    with ExitStack() as ph:
        psP = [psum(ph, "psPC%d" % i, [128, 512]) for i in range(2)]
        psS = [psum(ph, "psSC%d" % i, [128, 512]) for i in range(3)]
        psO = [psum(ph, "psOC%d" % i, [128, 512]) for i in range(2)]
        psT = psum(ph, "psTC", [128, 8, 128], BF16)
        qT = [sb(ph, "qC%d" % i, [128, S], BF16) for i in range(2)]
        iqT = [sb(ph, "iqC%d" % i, [128, S], BF16) for i in range(2)]
        kT = sb(ph, "kC", [128, S], BF16)
        ikT = sb(ph, "ikC", [128, S], BF16)
        vaug = sb(ph, "vaugC", [128, NT, 65], BF16)
        iw = sb(ph, "iwC", [128, NT, 4])
        pj = ExitStack()
        wt = [sb(pj, "wC0", [128, 8, 128], BF16)] * 2
        bq = sb(pj, "bqC", [128, 6])
        bv = sb(pj, "bvC", [128, 68])
        P.memset("pool", vaug[:], 1.0, w=["vaug"])
        jobs = [("c_q", 0, 128, qT[0], False), ("c_q", 128, 128, qT[1], False),
                ("c_iq", 0, 128, iqT[0], False), ("c_iq", 128, 128, iqT[1], False),
                ("c_k", 0, 64, kT, True), ("c_ik", 0, 64, ikT, True)]
        for i, (name, off, ncol, dst, dup) in enumerate(jobs):
            w_ = wt[i % 2]
            wk = KEY("wC", 0)
            load_w(w_, 0, name, off, ncol, wk)
            P.dma(bq[0:ncol, i:i + 1], bcol(name, off, ncol), w=["bq"])
            if dup:
                load_w(w_, 64, name, off, ncol, wk)
                P.dma(bq[64:128, i:i + 1], bcol(name, off, ncol), w=["bq"])

            def evac(tb, ps, pk, i=i, dst=dst):
                P.activation(dst[:, tb * 512:(tb + 1) * 512], ps[:, :], AF.Identity, bias=bq[:, i:i + 1],
                             r=[pk, "bq"], w=[KEY("fmC", i)])
            fm_proj(w_, wk, 128, psP, "psP", evac)
        load_w(wt[0], 0, "c_v", 0, 64, KEY("wC", 0))
        load_w(wt[0], 64, "c_iw", 0, 4, KEY("wC", 0))
        P.dma(bv[:, 0:64], brow("c_v", 0, 64), w=["bv"])
        P.dma(bv[:, 64:68], brow("c_iw", 0, 4), w=["bv"])

        def evac_v(t, ps, pk):
            P.tt("dve", vaug[:, t, 0:64], ps[:, 0:64], bv[:, 0:64], ALU.add, r=[pk, "bv"], w=["vaug"])
            P.tt("dve", iw[:, t, :], ps[:, 64:68], bv[:, 64:68], ALU.add, r=[pk, "bv"], w=["iw"])
        tm_proj(wt[0], KEY("wC", 0), 68, psP, "psP", evac_v)
        P.barrier()
        P.flush()
        pj.close()
        tab = sb(ph, "tabCs", [128, 4, TW_C], BF16)
        negtri = sb(ph, "negtriC", [128, 128])
        scs = [sb(ph, "scC%d" % i, [128, S]) for i in range(2)]
        junkD = sb(ph, "junkDC", [128, 832], BF16)
        sel = sb(ph, "selC", [128, S], BF16)
        maskT = sb(ph, "maskTC", [128, NT, 512], BF16)
        Ef = [sb(ph, "EfC%d" % i, [128, 512], BF16) for i in range(3)]
        Pb = [sb(ph, "PbC%d" % i, [128, 512], BF16) for i in range(3)]
        sm = sb(ph, "smC", [128, 8])
        bs = sb(ph, "bsC", [128, 8])
        NIT = 14
        sg = sb(ph, "sgC", [128, 24])
        junkA = sb(ph, "junkAC", [128, 1232], BF16)
        dk = sb(ph, "dkC", [128, 24])
        cn = sb(ph, "cnC", [128, 24])
        md = sb(ph, "mdC", [128, 24])
        p2row = sb(ph, "p2rowC", [128, 24])
        for k_ in range(NIT + 1):
            P.memset("pool", p2row[:, k_:k_ + 1], 2.0 ** -(k_ + 1), w=["p2row"])
        P.dma(negtri[:], dr["negtri"], w=["negtri"])
        for h in range(4):
            P.dma(tab[:, h, :], scr["tabC"][h], w=["tab"], eng="pool")
        P.memset("pool", maskT[:], 1.0, w=["maskT"])
        fmkeys = [KEY("fmC", i) for i in range(6)]
        npp = [0]

        def index_steps(qt):
            nk = 128 * (qt + 1)
            sc = scs[qt % 2]
            sck = KEY("sc", qt % 2)
            steps = []
            for s0 in range(0, nk, 512):
                wd = min(512, nk - s0)
                for hi in range(4):
                    def step(s0=s0, wd=wd, hi=hi):
                        pair = hi // 2
                        base = 64 * (hi % 2)
                        i2 = npp[0] % 2
                        npp[0] += 1
                        ps = psP[i2]
                        pk = KEY("psP", i2)
                        P.matmul(ps[:, 0:wd], iqT[pair][base:base + 64, qt * 128:(qt + 1) * 128], ikT[base:base + 64, s0:s0 + wd],
                                 r=fmkeys, w=[pk])
                        P.activation(ps[:, 0:wd], ps[:, 0:wd], AF.Relu, r=[pk], w=[pk])
                        if hi == 0:
                            P.ts("dve", sc[:, s0:s0 + wd], ps[:, 0:wd], iw[:, qt, 0:1], None, ALU.mult,
                                 r=[pk, "iw"], w=[sck])
                        else:
                            P.stt("dve", sc[:, s0:s0 + wd], ps[:, 0:wd], iw[:, qt, hi:hi + 1], sc[:, s0:s0 + wd], ALU.mult, ALU.add,
                                  r=[pk, "iw", sck], w=[sck])
                    steps.append(step)
            return steps

        tiles = list(range(2, NT))
        for st_ in index_steps(tiles[0]):
            st_()
        for b in range(4):
            for qi in range(4):
                qt = 4 * b + qi
                if qt < 2:
                    continue
                nk = 128 * (qt + 1)
                sc = scs[qt % 2]
                sck = KEY("sc", qt % 2)
                nxt = index_steps(qt + 1) if qt + 1 < NT else []
                per = -(-len(nxt) // NIT)
                P.op("dve", lambda e, nk=nk, sc=sc: e.tensor_reduce(bs[:, 0:1], sc[:, 0:nk], AX.X, ALU.min), r=[sck], w=["lo"])
                P.op("dve", lambda e, nk=nk, sc=sc: e.tensor_reduce(bs[:, 1:2], sc[:, 0:nk], AX.X, ALU.max), r=[sck], w=["hi"])
                P.tt("dve", bs[:, 2:3], bs[:, 1:2], bs[:, 0:1], ALU.subtract, r=["lo", "hi"], w=["d"])
                P.tt("dve", sc[:, qt * 128:nk], sc[:, qt * 128:nk], negtri[:], ALU.add, r=[sck, "negtri"], w=[sck])
                P.ts("dve", dk[:, 0:NIT + 1], p2row[:, 0:NIT + 1], bs[:, 2:3], None, ALU.mult, r=["d", "p2row"], w=["dk"])
                P.memset("dve", cn[:, 0:NIT], 0.0, w=["cn"])
                P.tt("dve", md[:, 0:1], bs[:, 0:1], dk[:, 0:1], ALU.add, r=["lo", "dk"], w=["md"])
                nd = (int(nk * 0.40) // 16) * 16
                na = nk - nd
                P.memset("dve", sg[:, 0:NIT], 0.0, w=["sg"])
                for it in range(NIT):
                    P.activation(junkA[:, 0:na], sc[:, nd:nk], AF.Sign, bias=md[:, it:it + 1], scale=-1.0,
                                 r=[sck, "md", "sg"], w=["junkA", "sg"], accum_out=sg[:, it:it + 1])
                    P.ts("dve", junkD[:, 0:nd], sc[:, 0:nd], md[:, it:it + 1], None, ALU.is_ge, ALU.add, r=[sck, "md", "cn"], w=["junkD", "cn"],
                         accum_out=cn[:, it:it + 1])
                    P.stt("dve", bs[:, 6:7], sg[:, it:it + 1], -0.5, cn[:, it:it + 1], ALU.mult, ALU.add, r=["sg", "cn"], w=["tc"])
                    P.ts("dve", bs[:, 5:6], bs[:, 6:7], 255.5 - na / 2.0, 0.5, ALU.is_ge, ALU.subtract, r=["tc"], w=["ge"])
                    P.stt("dve", md[:, it + 1:it + 2], bs[:, 5:6], dk[:, it:it + 1], md[:, it:it + 1], ALU.mult, ALU.add,
                          r=["ge", "dk", "md"], w=["md"])
                    for _ in range(per):
                        if nxt:
                            nxt.pop(0)()
                while nxt:
                    nxt.pop(0)()
                P.tt("dve", bs[:, 0:1], md[:, NIT:NIT + 1], dk[:, NIT:NIT + 1], ALU.subtract, r=["md", "dk"], w=["lo"])
                P.ts("dve", sel[:, 0:nk], sc[:, 0:nk], bs[:, 0:1], None, ALU.is_ge, r=[sck, "lo"], w=["sel"])
                for k0 in range(0, qt + 1, 8):
                    kn = min(8, qt + 1 - k0)
                    for j in range(kn):
                        P.transpose(psT[:, j, :], sel[:, (k0 + j) * 128:(k0 + j + 1) * 128], ident_b[:], r=["sel", "ident_b"], w=["psT"])
                    P.copy("act", maskT[:, k0:k0 + kn, qi * 128:(qi + 1) * 128], psT[:, 0:kn, :], r=["psT"], w=["maskT"])
            for h in range(4):
                pair = h // 2
                base = 64 * (h % 2)
                cnt = [0]

                def make_P(b_, kt, qs, w, ss, skey, h=h, cnt=cnt):
                    i = cnt[0] % 3
                    cnt[0] += 1
                    d0 = qs // 128 - kt
                    mk = maskT[:, kt, qs - 512 * b_:qs - 512 * b_ + w]
                    if kt >= 4 * b_ - 1:
                        P.activation(Ef[i][:, 0:w], ss[:, 0:w], AF.Exp, scale=0.125, r=[skey], w=[KEY("Ef", i)])
                        P.tt("dve", Ef[i][:, 0:w], Ef[i][:, 0:w], tab[:, h, 128 * d0:128 * d0 + w], ALU.mult,
                             r=[KEY("Ef", i), "tab"], w=[KEY("Ef", i)])
                        P.tt("dve", Pb[i][:, 0:w], Ef[i][:, 0:w], mk, ALU.mult, r=[KEY("Ef", i), "maskT"], w=[KEY("Pb", i)])
                    else:
                        P.activation(Ef[i][:, 0:w], ss[:, 0:w], AF.Exp, scale=0.125, bias=rbrow[:, 4 + h:5 + h],
                                     r=[skey, "rbrow"], w=[KEY("Ef", i)])
                        P.tt("dve", Pb[i][:, 0:w], Ef[i][:, 0:w], mk, ALU.mult, r=[KEY("Ef", i), "maskT"], w=[KEY("Pb", i)])
                    return Pb[i], KEY("Pb", i)

                def post(b_, ob, okey, h=h):
                    P.op("dve", lambda e: e.reciprocal(sm[:, 0:4].unsqueeze(2), ob[:, :, 64:65]), r=[okey], w=["sm0"])
                    P.tt("dve", mixed[:, 4 * b_:4 * b_ + 4, 512 + h * 64:512 + (h + 1) * 64], ob[:, :, 0:64],
                         sm[:, 0:4].unsqueeze(2).to_broadcast([128, 4, 64]), ALU.mult, r=[okey, "sm0"], w=[KEY("mixed", b_)])

                dense_attn(psS, psO,
                           lambda kt: kT[base:base + 64, kt * 128:(kt + 1) * 128],
                           lambda qs, w: qT[pair][base:base + 64, qs:qs + w],
                           lambda kt: vaug[:, kt, :],
                           lambda b_: list(range(0, 4 * b_ + 4)), make_P, post,
                           fmkeys + ["vaug"], blocks=[b])
        P.barrier()
        P.flush()


def _mixer_d(nc, P, dr, l, scr, sb, psum, dap, xbf, mixed, rbrow, ident_b, load_w, bcol, brow,
             fm_proj, tm_proj, dense_attn, KEY):
    with ExitStack() as ph:
        psP = [psum(ph, "psPD%d" % i, [128, 512]) for i in range(2)]
        psS = [psum(ph, "psSD%d" % i, [128, 512]) for i in range(2)]
        psO = [psum(ph, "psOD%d" % i, [128, 512]) for i in range(2)]
        psM = psum(ph, "psMD", [128, 512])
        qT = [sb(ph, "qD%d" % i, [128, S], BF16) for i in range(2)]
        ksT = sb(ph, "ksD", [128, S], BF16)
        kwT = sb(ph, "kwD", [128, S], BF16)
        vS = sb(ph, "vSD", [128, NT, 65], BF16)
        vW = sb(ph, "vWD", [128, NT, 65], BF16)
        gate = sb(ph, "gD", [128, NT, 12])
        kcmp = sb(ph, "kcmpD", [128, 128], BF16)
        vcmp = sb(ph, "vcmpD", [128, 65], BF16)
        acc = sb(ph, "accD", [128, NT, 256])
        selT = sb(ph, "selTD", [32, S], BF16)
        Ef = [sb(ph, "EfD%d" % i, [128, 512], BF16) for i in range(3)]
        Eb = [sb(ph, "EbD%d" % i, [128, 512], BF16) for i in range(3)]
        Pb = [sb(ph, "PbD%d" % i, [128, 512], BF16) for i in range(3)]
        sm = sb(ph, "smD", [128, 12])
        tmpD = sb(ph, "tmpD", [128, 4, 64])
        P.memset("pool", vS[:], 1.0, w=["vS"])
        P.memset("pool", vW[:], 1.0, w=["vW"])
        P.memset("pool", vcmp[:], 1.0, w=["vcmp"])
        with ExitStack() as p1:
            wt = [sb(p1, "wD%d" % i, [128, 8, 140], BF16) for i in range(2)]
            kvc = sb(p1, "kvcD", [128, S], BF16)
            bq = sb(p1, "bqD", [128, 6])
            bv = sb(p1, "bvD", [128, 140])
            W1 = sb(p1, "W1D", [128, 32, 256], BF16)
            w2k = sb(p1, "w2kD", [128, 2, 128], BF16)
            w2v = sb(p1, "w2vD", [128, 2, 64], BF16)
            posT = sb(p1, "posTD", [128, 32], BF16)
            hs = sb(p1, "hsD", [128, 2, 2, 128], BF16)
            hb = sb(p1, "hbD", [128, 4])
            psX = psum(p1, "psXD", [128, 512])
            jobs = [("d_q", 0, 128, qT[0], 0), ("d_q", 128, 128, qT[1], 0),
                    ("d_kc", 0, 128, kvc, 0), ("d_ks", 0, 64, ksT, 1), ("d_kw", 0, 64, kwT, 1)]
            for i, (name, off, ncol, dst, dup) in enumerate(jobs):
                w_ = wt[i % 2]
                wk = KEY("wD", i % 2)
                load_w(w_, 0, name, off, ncol, wk)
                P.dma(bq[0:ncol, i:i + 1], bcol(name, off, ncol), w=["bq"])
                if dup:
                    load_w(w_, 64, name, off, ncol, wk)
                    P.dma(bq[64:128, i:i + 1], bcol(name, off, ncol), w=["bq"])

                def evac(tb, ps, pk, i=i, dst=dst):
                    P.activation(dst[:, tb * 512:(tb + 1) * 512], ps[:, :], AF.Identity, bias=bq[:, i:i + 1],
                                 r=[pk, "bq"], w=[KEY("fmD", i)])
                fm_proj(w_, wk, 128, psP, "psP", evac)
            load_w(wt[0], 0, "d_vs", 0, 64, KEY("wD", 0))
            load_w(wt[0], 64, "d_vw", 0, 64, KEY("wD", 0))
            load_w(wt[0], 128, "d_g", 0, 12, KEY("wD", 0))
            P.dma(bv[:, 0:64], brow("d_vs", 0, 64), w=["bv"])
            P.dma(bv[:, 64:128], brow("d_vw", 0, 64), w=["bv"])
            P.dma(bv[:, 128:140], brow("d_g", 0, 12), w=["bv"])

            def evac_v(t, ps, pk):
                P.tt("dve", vS[:, t, 0:64], ps[:, 0:64], bv[:, 0:64], ALU.add, r=[pk, "bv"], w=["vS"])
                P.tt("dve", vW[:, t, 0:64], ps[:, 64:128], bv[:, 64:128], ALU.add, r=[pk, "bv"], w=["vW"])
                P.tt("dve", ps[:, 128:140], ps[:, 128:140], bv[:, 128:140], ALU.add, r=[pk, "bv"], w=[pk])
                P.activation(gate[:, t, :], ps[:, 128:140], AF.Sigmoid, r=[pk], w=["gate"])
            tm_proj(wt[0], KEY("wD", 0), 140, psP, "psP", evac_v)
            for wh in range(2):
                P.dma(W1[64 * wh:64 * wh + 64, :, :], dr["d_cmp_w1"][l, wh].rearrange("(l d) j -> d l j", d=64), w=["W1"], eng="pool")
                P.dma(posT[64 * wh:64 * wh + 64, :], dr["d_cmp_pos"][l, wh], w=["posT"], eng="pool")
            for c2 in range(2):
                P.dma(w2k[:, :, 64 * c2:64 * c2 + 64], dr["d_cmp_w2"][l, 0].rearrange("(h p) n -> p h n", p=128), w=["w2k"], eng="pool")
            P.dma(w2v[:, :, :], dr["d_cmp_w2"][l, 1].rearrange("(h p) n -> p h n", p=128), w=["w2v"], eng="pool")
            kv3 = kvc[:].rearrange("p (c s) -> p c s", s=16)
            for wh in range(2):
                base = 64 * wh
                for half in range(2):
                    ps = psP[half]
                    pk = KEY("psP", half)
                    for li in range(32):
                        P.matmul(ps[:, 0:127], W1[base:base + 64, li, half * 128:(half + 1) * 128],
                                 kv3[base:base + 64, (li // 16):(li // 16) + 127, li % 16],
                                 start=(li == 0), stop=(li == 31), r=["W1", KEY("fmD", 2)], w=[pk])
                    for li in range(32):
                        P.matmul(psX[:, 0:1], W1[base:base + 64, li, half * 128:(half + 1) * 128], posT[base:base + 64, li:li + 1],
                                 start=(li == 0), stop=(li == 31), r=["W1", "posT"], w=["psX"])
                    P.copy("dve", hb[:, 2 * wh + half:2 * wh + half + 1], psX[:, 0:1], r=["psX"], w=["hb"])
                    P.activation(hs[:, wh, half, 0:127], ps[:, 0:127], AF.Silu, bias=hb[:, 2 * wh + half:2 * wh + half + 1],
                                 r=[pk, "hb"], w=["hs"])
            for half in range(2):
                P.matmul(psM[:, 0:127], w2k[:, half, :], hs[:, 0, half, 0:127], start=(half == 0), stop=(half == 1), r=["w2k", "hs"], w=["psM"])
            P.copy("dve", kcmp[:, 0:127], psM[:, 0:127], r=["psM"], w=["kcmp"])
            for half in range(2):
                P.matmul(psX[0:127, 0:64], hs[:, 1, half, 0:127], w2v[:, half, :], start=(half == 0), stop=(half == 1), r=["w2v", "hs"], w=["psX"])
            P.copy("dve", vcmp[0:127, 0:64], psX[0:127, 0:64], r=["psX"], w=["vcmp"])
            P.barrier()
            P.flush()
        with ExitStack() as p2:
            tabm = sb(p2, "tabmD", [128, 4, S], BF16)
            tabm_ops = [P.dma(tabm[:, h, :], scr["tabM"][h]) for h in range(4)]
            ovx = sb(p2, "ovxD", [128, 33], BF16)
            selF = sb(p2, "selFD", [128, NT, 32])
            selA = sb(p2, "selAD", [128, NT, 32])
            selN = sb(p2, "selND", [128, NT, 32])
            imp = sb(p2, "impD", [128, NT, 32])
            v1 = sb(p2, "v1D", [128, 32])
            v2 = sb(p2, "v2D", [128, 32])
            m8 = sb(p2, "m8D", [128, 16])
            selm = sb(p2, "selmD", [128, 32], BF16)
            P.dma(ovx[0:127, :], dr["ovx"], w=["ovx"], eng="pool")
            P.dma(selF[:], dr["selF"].rearrange("p (t j) -> p t j", j=32), w=["selF"])
            P.dma(selA[:], dr["selA"].rearrange("p (t j) -> p t j", j=32), w=["selA"])
            P.dma(selN[:], dr["selN"].rearrange("p (t j) -> p t j", j=32), w=["selN"])
            psTb = psum(p2, "psTbD", [128, 1024], BF16)
            for b in range(4):
                ob = psO[b % 2][:, 0:260].rearrange("p (q d) -> p q d", q=4)
                okey = KEY("psO", b % 2)
                o2 = psM[:, 0:132].rearrange("p (q d) -> p q d", q=4)
                for h in range(4):
                    pair = h // 2
                    base = 64 * (h % 2)
                    ss = psS[h % 2]
                    skey = KEY("psS", h % 2)
                    i = h % 2
                    P.matmul(ss[0:127, :], kcmp[base:base + 64, 0:127], qT[pair][base:base + 64, 512 * b:512 * (b + 1)], r=["kcmp"], w=[skey])
                    P.activation(Ef[i][0:127, :], ss[0:127, :], AF.Exp, scale=0.125, r=[skey], w=[KEY("Ef", i)])
                    P.tt("dve", Pb[i][0:127, :], Ef[i][0:127, :], tabm[0:127, h, 512 * b:512 * (b + 1)], ALU.mult, r=[KEY("Ef", i)], w=[KEY("Pb", i)], after=tabm_ops)
                    for qi in range(4):
                        P.matmul(ob[:, qi, :], Pb[i][0:127, qi * 128:(qi + 1) * 128], vcmp[0:127, :], start=(qi == 0), stop=True,
                                 r=[KEY("Pb", i), "vcmp"], w=[okey], sgc=True)
                    for qi in range(4):
                        P.matmul(o2[:, qi, :], Pb[i][0:127, qi * 128:(qi + 1) * 128], ovx[0:127, :], start=(qi == 0), stop=True,
                                 r=[KEY("Pb", i), "ovx"], w=["psM"], sgc=True)
                    bsl = slice(4 * b, 4 * b + 4)
                    P.ts("dve", sm[:, 0:4].unsqueeze(2), ob[:, :, 64:65], 1e-30, None, ALU.max, r=[okey], w=["sm0"])
                    P.op("dve", lambda e: e.reciprocal(sm[:, 4:8], sm[:, 0:4]), r=["sm0"], w=["sm1"])
                    P.tt("dve", sm[:, 8:12], sm[:, 4:8], gate[:, bsl, 3 * h], ALU.mult, r=["sm1", "gate"], w=["sm2"])
                    P.tt("dve", acc[:, bsl, h * 64:(h + 1) * 64], ob[:, :, 0:64], sm[:, 8:12].unsqueeze(2).to_broadcast([128, 4, 64]), ALU.mult,
                         r=[okey, "sm2"], w=[KEY("acc", b)])
                    if h == 0:
                        P.tt("dve", imp[:, bsl, :], o2[:, :, 0:32], sm[:, 4:8].unsqueeze(2).to_broadcast([128, 4, 32]), ALU.mult,
                             r=["psM", "sm1"], w=["imp"])
                    else:
                        P.tt("dve", tmpD[:, :, 0:32], o2[:, :, 0:32], sm[:, 4:8].unsqueeze(2).to_broadcast([128, 4, 32]), ALU.mult,
                             r=["psM", "sm1"], w=["tmpD"])
                        P.tt("dve", imp[:, bsl, :], imp[:, bsl, :], tmpD[:, :, 0:32], ALU.add, r=["tmpD", "imp"], w=["imp"])
                for qi in range(4):
                    qt = 4 * b + qi
                    P.tt("dve", v1[:], imp[:, qt, :], selF[:, qt, :], ALU.max, r=["imp", "selF"], w=["v1"])
                    P.tt("dve", v1[:], v1[:], selA[:, qt, :], ALU.mult, r=["v1", "selA"], w=["v1"])
                    P.tt("dve", v1[:], v1[:], selN[:, qt, :], ALU.add, r=["v1", "selN"], w=["v1"])
                    P.op("dve", lambda e: e.max(out=m8[:, 0:8], in_=v1[:]), r=["v1"], w=["m8a"])
                    P.op("dve", lambda e: e.match_replace(out=v2[:], in_to_replace=m8[:, 0:8], in_values=v1[:], imm_value=-3.0e38),
                         r=["v1", "m8a"], w=["v2"])
                    P.op("dve", lambda e: e.max(out=m8[:, 8:16], in_=v2[:]), r=["v2"], w=["m8b"])
                    P.ts("dve", selm[:], v1[:], m8[:, 15:16], None, ALU.is_ge, r=["v1", "m8b"], w=["selm"])
                    P.transpose(psTb[0:32, 0:128], selm[:], ident_b[:], r=["selm", "ident_b"], w=["psTb"])
                    P.copy("act", selT[:, qt * 128:(qt + 1) * 128], psTb[0:32, 0:128], r=["psTb"], w=["selT"])
            P.barrier()
            P.flush()
        with ExitStack() as p3:
            tab = sb(p3, "tabSs", [128, 4, TW_C], BF16)
            ind = sb(p3, "indD", [32, NT, 128], BF16)
            psX3 = psum(p3, "psX3", [128, 512])
            pmring = [psM, psP[0], psP[1]]
            P.dma(ind[:], dr["ind"].rearrange("p (t k) -> p t k", k=128), w=["ind"], eng="pool")
            for h in range(4):
                P.dma(tab[:, h, :], scr["tabS"][h], w=["tab"], eng="pool")
            for h in range(4):
                pair = h // 2
                base = 64 * (h % 2)
                cnt = [0]

                mcnt = [0]

                def prep(b_, kt, qs, w, mcnt=mcnt):
                    j = mcnt[0] % 3
                    mcnt[0] += 1
                    pm = pmring[j]
                    mkey = KEY("psMr", j)
                    P.matmul(pm[:, 0:w], ind[:, kt, :], selT[:, qs:qs + w], r=["ind", "selT"], w=[mkey])
                    return pm, mkey

                def make_P(b_, kt, qs, w, ss, skey, h=h, cnt=cnt, aux=None):
                    i = cnt[0] % 3
                    cnt[0] += 1
                    d0 = qs // 128 - kt
                    pm, mkey = aux
                    if kt >= 4 * b_ - 1:
                        P.activation(Ef[i][:, 0:w], ss[:, 0:w], AF.Exp, scale=0.125, r=[skey], w=[KEY("Ef", i)])
                        P.tt("dve", Ef[i][:, 0:w], Ef[i][:, 0:w], tab[:, h, 128 * d0:128 * d0 + w], ALU.mult,
                             r=[KEY("Ef", i), "tab"], w=[KEY("Ef", i)])
                        P.tt("dve", Pb[i][:, 0:w], Ef[i][:, 0:w], pm[:, 0:w], ALU.mult, r=[KEY("Ef", i), mkey], w=[KEY("Pb", i)])
                    else:
                        P.activation(Eb[i][:, 0:w], ss[:, 0:w], AF.Exp, scale=0.125, bias=rbrow[:, 8 + h:9 + h],
                                     r=[skey, "rbrow"], w=[KEY("Eb", i)])
                        P.tt("dve", Pb[i][:, 0:w], Eb[i][:, 0:w], pm[:, 0:w], ALU.mult, r=[KEY("Eb", i), mkey], w=[KEY("Pb", i)])
                    return Pb[i], KEY("Pb", i)

                def post(b_, ob, okey, h=h):
                    bsl = slice(4 * b_, 4 * b_ + 4)
                    P.op("dve", lambda e: e.reciprocal(sm[:, 4:8].unsqueeze(2), ob[:, :, 64:65]), r=[okey], w=["sm1"])
                    P.tt("dve", sm[:, 8:12], sm[:, 4:8], gate[:, bsl, 3 * h + 1], ALU.mult, r=["sm1", "gate"], w=["sm2"])
                    P.tt("dve", tmpD[:], ob[:, :, 0:64], sm[:, 8:12].unsqueeze(2).to_broadcast([128, 4, 64]), ALU.mult,
                         r=[okey, "sm2"], w=["tmpD"])
                    P.tt("pool", acc[:, bsl, h * 64:(h + 1) * 64], acc[:, bsl, h * 64:(h + 1) * 64], tmpD[:], ALU.add,
                         r=["tmpD", KEY("acc", b_)], w=[KEY("acc", b_)])

                dense_attn(psS + [psX3], psO,
                           lambda kt: ksT[base:base + 64, kt * 128:(kt + 1) * 128],
                           lambda qs, w: qT[pair][base:base + 64, qs:qs + w],
                           lambda kt: vS[:, kt, :],
                           lambda b_: list(range(0, 4 * b_ + 4)), make_P, post, ["vS"], prep=prep)
            P.barrier()
            P.flush()
        with ExitStack() as p4:
            tab = sb(p4, "tabWs", [128, 4, TW_W], BF16)
            for h in range(4):
                P.dma(tab[:, h, :], scr["tabW"][h], w=["tab"], eng="pool")
            for h in range(4):
                pair = h // 2
                base = 64 * (h % 2)
                cnt = [0]

                def make_P(b_, kt, qs, w, ss, skey, h=h, cnt=cnt):
                    i = cnt[0] % 3
                    cnt[0] += 1
                    d0 = qs // 128 - kt
                    P.activation(Ef[i][:, 0:w], ss[:, 0:w], AF.Exp, scale=0.125, r=[skey], w=[KEY("Ef", i)])
                    P.tt("dve", Pb[i][:, 0:w], Ef[i][:, 0:w], tab[:, h, 128 * d0:128 * d0 + w], ALU.mult,
                         r=[KEY("Ef", i), "tab"], w=[KEY("Pb", i)])
                    return Pb[i], KEY("Pb", i)

                def post(b_, ob, okey, h=h):
                    bsl = slice(4 * b_, 4 * b_ + 4)
                    P.op("dve", lambda e: e.reciprocal(sm[:, 4:8].unsqueeze(2), ob[:, :, 64:65]), r=[okey], w=["sm1"])
                    P.tt("dve", sm[:, 8:12], sm[:, 4:8], gate[:, bsl, 3 * h + 2], ALU.mult, r=["sm1", "gate"], w=["sm2"])
                    P.tt("dve", tmpD[:], ob[:, :, 0:64], sm[:, 8:12].unsqueeze(2).to_broadcast([128, 4, 64]), ALU.mult,
                         r=[okey, "sm2"], w=["tmpD"])
                    P.tt("pool", mixed[:, bsl, 768 + h * 64:768 + (h + 1) * 64], acc[:, bsl, h * 64:(h + 1) * 64], tmpD[:], ALU.add,
                         r=["tmpD", KEY("acc", b_)], w=[KEY("mixed", b_)])

                dense_attn(psS + psP, psO,
                           lambda kt: kwT[base:base + 64, kt * 128:(kt + 1) * 128],
                           lambda qs, w: qT[pair][base:base + 64, qs:qs + w],
                           lambda kt: vW[:, kt, :],
                           lambda b_: list(range(max(0, 4 * b_ - 4), 4 * b_ + 4)), make_P, post, ["vW"])
            P.barrier()
            P.flush()


_NC_CACHE = {}


def layout_weights(inputs):
    out = {}
    for k in WEIGHT_SHAPES:
        v = np.asarray(inputs[k], dtype=np.float32)
        if k == "a_conv":
            v = v.transpose(0, 2, 1)
        elif k == "d_cmp_pos":
            v = v.transpose(0, 1, 3, 2)
        elif k in ("b_out", "ln1_g", "ln1_b", "b_ff2", "ln2_g", "ln2_b", "b_ff1"):
            v = v.reshape(2, -1, 128).transpose(0, 2, 1)
        out[k] = np.ascontiguousarray(v)
    return out


def kernel(**inputs):
    n_cores = 8
    x = np.ascontiguousarray(inputs["x"], dtype=np.float32)
    consts = _host_consts()
    if "nc" not in _NC_CACHE:
        _NC_CACHE["nc"] = build_nc()
    nc = _NC_CACHE["nc"]
    in_maps = []
    wl = layout_weights(inputs)
    for c in range(n_cores):
        m = {"x": np.ascontiguousarray(x[2 * c:2 * c + 2])}
        for k in WEIGHT_SHAPES:
            m[k] = wl[k]
        for k, v in consts.items():
            m["k_" + k] = v
        in_maps.append(m)
    res = run_bass_kernel_spmd(nc, in_maps, core_ids=list(range(n_cores)))
    return np.concatenate([np.asarray(r["out"]) for r in res.results], axis=0).astype(np.float32)
```
